# Optimizing a Trainium2 kernel written in Bass

```python
import math
import jax, jax.numpy as jnp
from jax import lax
import numpy as np

D_MODEL = 1024
BATCH = 8
SEQ = 8192
DEPTH = 2

NUM_META = 16
MIX_WIDTH = D_MODEL
ATTN_HEADS = D_MODEL // 256
QK_NOPE_DIM = 128
QK_ROPE_DIM = 64
V_HEAD_DIM = 128
Q_LORA_RANK = 3 * D_MODEL // 8
KV_LORA_RANK = D_MODEL // 4
ATTN_WIDTH = ATTN_HEADS * V_HEAD_DIM
ATTN_SCALE = 1.0 / math.sqrt(QK_NOPE_DIM + QK_ROPE_DIM)
ROPE_THETA = 10000.0
Q_BLOCK = 128
SSM_WIDTH = MIX_WIDTH - ATTN_WIDTH
SSM_GROUP = 16
SSM_GROUPS = SSM_WIDTH // SSM_GROUP
SSM_STATE = 64
SSM_CHUNK = 128
DT_MIN = 0.001
DT_MAX = 0.1
IN_WIDTH = Q_LORA_RANK + KV_LORA_RANK + QK_ROPE_DIM + SSM_WIDTH
FFN_HIDDEN = -(-8 * D_MODEL // (3 * 256)) * 256
RMS_EPS = 1e-6

kernel_name = "hymba_mla_s5_hybrid_block"


def _rms_norm(x, g, eps=RMS_EPS):
    xf = x.astype(jnp.float32)
    y = xf * lax.rsqrt(jnp.mean(xf * xf, axis=-1, keepdims=True) + eps)
    return (y * g.astype(jnp.float32)).astype(x.dtype)


def _rope(x, cos, sin):
    half = x.shape[-1] // 2
    x1, x2 = x[..., :half], x[..., half:]
    return jnp.concatenate([x1 * cos - x2 * sin, x2 * cos + x1 * sin], axis=-1).astype(x.dtype)


def _attend(q_nope, q_rope, k_nope, k_rope, v, q_pos):
    s = (jnp.einsum('bthd,bshd->bhts', q_nope, k_nope)
         + jnp.einsum('bthr,bsr->bhts', q_rope, k_rope)).astype(jnp.float32) * ATTN_SCALE
    k_pos = jnp.arange(k_nope.shape[1])
    s = jnp.where(k_pos[None, :] <= q_pos[:, None], s, -jnp.inf)
    p = jax.nn.softmax(s, axis=-1).astype(v.dtype)
    return jnp.einsum('bhts,bshd->bthd', p, v)


def _mla_mixer(c_q, c_kv, k_rope_in, q_norm_g, w_uq, kv_norm_g, w_ukv, cos, sin):
    b, L, _ = c_q.shape
    M = NUM_META
    q = (_rms_norm(c_q, q_norm_g) @ w_uq).reshape(b, L, ATTN_HEADS, QK_NOPE_DIM + QK_ROPE_DIM)
    q_nope = q[..., :QK_NOPE_DIM]
    q_rope = _rope(q[..., QK_NOPE_DIM:], cos[:, None], sin[:, None])
    kv = (_rms_norm(c_kv, kv_norm_g) @ w_ukv).reshape(b, L, ATTN_HEADS, QK_NOPE_DIM + V_HEAD_DIM)
    k_nope, v = kv[..., :QK_NOPE_DIM], kv[..., QK_NOPE_DIM:]
    k_rope = _rope(k_rope_in, cos, sin)
    pos = jnp.arange(L)
    out_meta = _attend(q_nope[:, :M], q_rope[:, :M], k_nope[:, :M], k_rope[:, :M], v[:, :M], pos[:M])
    n = (L - M) // Q_BLOCK

    def blocks(t):
        return jnp.moveaxis(t[:, M:].reshape(b, n, Q_BLOCK, *t.shape[2:]), 1, 0)

    out_real = lax.map(lambda a: _attend(a[0], a[1], k_nope, k_rope, v, a[2]),
                       (blocks(q_nope), blocks(q_rope), pos[M:].reshape(n, Q_BLOCK)))
    out_real = jnp.moveaxis(out_real, 0, 1).reshape(b, L - M, ATTN_HEADS, V_HEAD_DIM)
    out = jnp.concatenate([out_meta, out_real], axis=1)
    return out.reshape(b, L, ATTN_WIDTH)


def _cplx_linear_op(e1, e2):
    a1r, a1i, b1r, b1i = e1
    a2r, a2i, b2r, b2i = e2
    ar = a2r * a1r - a2i * a1i
    ai = a2r * a1i + a2i * a1r
    br = a2r * b1r - a2i * b1i + b2r
    bi = a2r * b1i + a2i * b1r + b2i
    return ar, ai, br, bi


def _ssm_mixer(u, a_re, a_im, log_dt, b_re, b_im, c_re, c_im, d_skip, w_glu):
    f32 = jnp.float32
    b, L, _ = u.shape
    M = NUM_META
    lr = jnp.minimum(a_re.astype(f32), -1e-4)
    li = a_im.astype(f32)
    dt = jnp.exp(log_dt.astype(f32))[:, None]
    mag = jnp.exp(lr * dt)
    lam_re = mag * jnp.cos(li * dt)
    lam_im = mag * jnp.sin(li * dt)
    nr, ni = lam_re - 1.0, lam_im
    den = lr * lr + li * li
    coef_re = ((nr * lr + ni * li) / den)[..., None]
    coef_im = ((ni * lr - nr * li) / den)[..., None]
    br, bi = b_re.astype(f32), b_im.astype(f32)
    bbar_re = coef_re * br - coef_im * bi
    bbar_im = coef_re * bi + coef_im * br
    cr_w, ci_w = c_re.astype(f32), c_im.astype(f32)

    def chunk(carry, u_c):
        sr0, si0 = carry
        bu_re = jnp.einsum('btgh,gph->btgp', u_c, bbar_re)
        bu_im = jnp.einsum('btgh,gph->btgp', u_c, bbar_im)
        a_r = jnp.broadcast_to(lam_re, bu_re.shape)
        a_i = jnp.broadcast_to(lam_im, bu_im.shape)
        acr, aci, xr, xi = lax.associative_scan(_cplx_linear_op, (a_r, a_i, bu_re, bu_im), axis=1)
        sr = xr + acr * sr0[:, None] - aci * si0[:, None]
        si = xi + acr * si0[:, None] + aci * sr0[:, None]
        y = jnp.einsum('btgp,ghp->btgh', sr, cr_w) - jnp.einsum('btgp,ghp->btgh', si, ci_w)
        return (sr[:, -1], si[:, -1]), y

    uf = u.astype(f32).reshape(b, L, SSM_GROUPS, SSM_GROUP)
    zeros = jnp.zeros((b, SSM_GROUPS, SSM_STATE), f32)
    carry, y_meta = chunk((zeros, zeros), uf[:, :M])
    n = (L - M) // SSM_CHUNK
    u_chunks = jnp.moveaxis(uf[:, M:].reshape(b, n, SSM_CHUNK, SSM_GROUPS, SSM_GROUP), 1, 0)
    _, y_real = lax.scan(chunk, carry, u_chunks)
    y_real = jnp.moveaxis(y_real, 0, 1).reshape(b, L - M, SSM_WIDTH)
    y = jnp.concatenate([y_meta.reshape(b, M, SSM_WIDTH), y_real], axis=1) + d_skip.astype(f32) * uf.reshape(b, L, SSM_WIDTH)
    g = jax.nn.gelu(y).astype(u.dtype)
    return g * jax.nn.sigmoid(g @ w_glu)


def setup_inputs(seed: int = 0) -> dict:
    key = jax.random.key(seed)
    ks = jax.random.split(key, 32)
    f32 = jnp.float32

    def nrm(k, shape, scale):
        return jax.random.normal(k, shape, f32) * scale

    def gain(k, shape):
        return 1.0 + 0.02 * jax.random.normal(k, shape, f32)

    G, P, HC = SSM_GROUPS, SSM_STATE, SSM_GROUP
    n_idx = jnp.arange(P, dtype=f32)
    return {
        "x": nrm(ks[0], (BATCH, SEQ, D_MODEL), 1.0),
        "meta_tokens": nrm(ks[1], (NUM_META, D_MODEL), 1.0),
        "norm_mix_g": gain(ks[2], (DEPTH, D_MODEL)),
        "w_in": nrm(ks[3], (DEPTH, D_MODEL, IN_WIDTH), D_MODEL ** -0.5),
        "q_norm_g": gain(ks[4], (DEPTH, Q_LORA_RANK)),
        "w_uq": nrm(ks[5], (DEPTH, Q_LORA_RANK, ATTN_HEADS * (QK_NOPE_DIM + QK_ROPE_DIM)), Q_LORA_RANK ** -0.5),
        "kv_norm_g": gain(ks[6], (DEPTH, KV_LORA_RANK)),
        "w_ukv": nrm(ks[7], (DEPTH, KV_LORA_RANK, ATTN_HEADS * (QK_NOPE_DIM + V_HEAD_DIM)), KV_LORA_RANK ** -0.5),
        "ssm_a_re": -0.5 + 0.01 * jax.random.normal(ks[8], (DEPTH, G, P), f32),
        "ssm_a_im": math.pi * n_idx + 0.01 * jax.random.normal(ks[9], (DEPTH, G, P), f32),
        "ssm_log_dt": jax.random.uniform(ks[10], (DEPTH, G), f32, math.log(DT_MIN), math.log(DT_MAX)),
        "ssm_b_re": nrm(ks[11], (DEPTH, G, P, HC), (2 * HC) ** -0.5),
        "ssm_b_im": nrm(ks[12], (DEPTH, G, P, HC), (2 * HC) ** -0.5),
        "ssm_c_re": nrm(ks[13], (DEPTH, G, HC, P), P ** -0.5),
        "ssm_c_im": nrm(ks[14], (DEPTH, G, HC, P), P ** -0.5),
        "ssm_d": nrm(ks[15], (DEPTH, SSM_WIDTH), 1.0),
        "w_glu": nrm(ks[16], (DEPTH, SSM_WIDTH, SSM_WIDTH), SSM_WIDTH ** -0.5),
        "attn_out_g": gain(ks[17], (DEPTH, ATTN_WIDTH)),
        "ssm_out_g": gain(ks[18], (DEPTH, SSM_WIDTH)),
        "w_o": nrm(ks[19], (DEPTH, MIX_WIDTH, D_MODEL), MIX_WIDTH ** -0.5),
        "norm_ffn_g": gain(ks[20], (DEPTH, D_MODEL)),
        "w_gate": nrm(ks[21], (DEPTH, D_MODEL, FFN_HIDDEN), D_MODEL ** -0.5),
        "w_up": nrm(ks[22], (DEPTH, D_MODEL, FFN_HIDDEN), D_MODEL ** -0.5),
        "w_down": nrm(ks[23], (DEPTH, FFN_HIDDEN, D_MODEL), FFN_HIDDEN ** -0.5),
        "final_norm_g": gain(ks[24], (D_MODEL,)),
    }


def reference(x, meta_tokens, norm_mix_g, w_in, q_norm_g, w_uq, kv_norm_g, w_ukv,
              ssm_a_re, ssm_a_im, ssm_log_dt, ssm_b_re, ssm_b_im, ssm_c_re, ssm_c_im,
              ssm_d, w_glu, attn_out_g, ssm_out_g, w_o, norm_ffn_g, w_gate, w_up, w_down,
              final_norm_g):
    b = x.shape[0]
    M = NUM_META
    h = jnp.concatenate([jnp.broadcast_to(meta_tokens[None].astype(x.dtype), (b, M, D_MODEL)), x], axis=1)
    L = h.shape[1]
    pos_f = jnp.arange(L, dtype=jnp.float32)
    inv_freq = 1.0 / (ROPE_THETA ** (jnp.arange(0, QK_ROPE_DIM, 2, dtype=jnp.float32) / QK_ROPE_DIM))
    ang = pos_f[:, None] * inv_freq[None, :]
    cos, sin = jnp.cos(ang), jnp.sin(ang)
    splits = [Q_LORA_RANK, Q_LORA_RANK + KV_LORA_RANK, Q_LORA_RANK + KV_LORA_RANK + QK_ROPE_DIM]
    for l in range(DEPTH):
        hn = _rms_norm(h, norm_mix_g[l])
        z = hn @ w_in[l]
        c_q, c_kv, k_r, u = jnp.split(z, splits, axis=-1)
        attn = _mla_mixer(c_q, c_kv, k_r, q_norm_g[l], w_uq[l], kv_norm_g[l], w_ukv[l], cos, sin)
        ssm = _ssm_mixer(u, ssm_a_re[l], ssm_a_im[l], ssm_log_dt[l], ssm_b_re[l], ssm_b_im[l],
                         ssm_c_re[l], ssm_c_im[l], ssm_d[l], w_glu[l])
        mixed = jnp.concatenate([_rms_norm(attn, attn_out_g[l]), _rms_norm(ssm, ssm_out_g[l])], axis=-1)
        h = h + mixed @ w_o[l]
        hn = _rms_norm(h, norm_ffn_g[l])
        h = h + (jax.nn.silu(hn @ w_gate[l]) * (hn @ w_up[l])) @ w_down[l]
    return _rms_norm(h, final_norm_g)[:, M:]
```

```python
import math, os
from contextlib import ExitStack
import numpy as np
import concourse.bass as bass
import concourse.mybir as mybir
from concourse.bass_utils import run_bass_kernel_spmd

F32 = mybir.dt.float32
BF16 = mybir.dt.bfloat16
ALU = mybir.AluOpType
AF = mybir.ActivationFunctionType

D = 1024; DC = 8; NM = 16; INW = 1216; QL = 384; KVL = 256; DR = 64; SW = 512
H = 4; DN = 128; DV = 128; FF = 2816; FC = 22; DEPTH = 2
EPS = 1e-6
SCALE = 1.0 / math.sqrt(192.0)
TS = 512


class Rec:
    ENG = ["pe", "act", "dve", "pool", "sp"]

    def __init__(self):
        self.ins = []
        self.lastw = {}
        self.readers = {}
        self.dmacnt = {}
        self.last_eng = {}
        self.last_dma = {}

    def _add(self, eng, fn, reads, writes, dma):
        d = set()
        for k in reads:
            if k in self.lastw:
                d.add(self.lastw[k])
        for k in writes:
            if k in self.lastw:
                d.add(self.lastw[k])
            d.update(self.readers.get(k, ()))
        i = len(self.ins)
        val = None
        if dma is not None:
            self.dmacnt[dma] = self.dmacnt.get(dma, 0) + 16
            val = self.dmacnt[dma]
        d |= set(self._extra)
        self.ins.append(dict(eng=eng, fn=fn, deps=d, dma=dma, val=val))
        if dma is not None:
            self.last_dma[dma] = i
        else:
            self.last_eng[eng] = i
        for k in reads:
            self.readers.setdefault(k, []).append(i)
        for k in writes:
            self.lastw[k] = i
            self.readers[k] = []
        return i

    def op(self, eng, fn, reads=(), writes=()):
        return self._add(eng, fn, reads, writes, None)

    def dma(self, eng, out, in_, sem, reads=(), writes=()):
        return self._add(eng, lambda e, o=out, i=in_: e.dma_start(out=o, in_=i), reads, writes, sem)

    _extra = ()

    def barrier(self):
        deps = list(self.last_eng.values()) + list(self.last_dma.values())
        self._extra = deps
        for e in self.ENG:
            self.op(e, lambda en: en.nop())
        self._extra = ()

    def emit(self, nc, stack):
        ins = self.ins
        has_cons = [False] * len(ins)
        for x in ins:
            for d in x["deps"]:
                has_cons[d] = True
        engsem = {e: stack.enter_context(nc.semaphore("es_" + e)) for e in self.ENG}
        dmasem = {k: stack.enter_context(nc.semaphore("ds_" + k)) for k in self.dmacnt}
        cnt = {e: 0 for e in self.ENG}
        for i, x in enumerate(ins):
            if x["dma"] is None and has_cons[i]:
                cnt[x["eng"]] += 1
                x["val"] = cnt[x["eng"]]
        per = {e: [i for i, x in enumerate(ins) if x["eng"] == e] for e in self.ENG}
        outer = self

        def run(engname, eng):
            seen = {}
            for i in per[engname]:
                x = ins[i]
                waits = {}
                for d in x["deps"]:
                    p = ins[d]
                    if p["dma"] is not None:
                        key = ("d", p["dma"])
                    else:
                        if p["eng"] == "pe" and engname == "pe" and x["dma"] is None:
                            continue
                        key = ("e", p["eng"])
                    if p["val"] > waits.get(key, 0):
                        waits[key] = p["val"]
                for key, v in waits.items():
                    if seen.get(key, 0) >= v:
                        continue
                    seen[key] = v
                    sem = dmasem[key[1]] if key[0] == "d" else engsem[key[1]]
                    eng.wait_ge(sem, v)
                r = x["fn"](eng)
                if x["dma"] is not None:
                    r.then_inc(dmasem[x["dma"]], 16)
                elif has_cons[i]:
                    r.then_inc(engsem[engname], 1)
            return seen

        block = stack.enter_context(nc.Block())

        @block.tensor
        def _(e):
            run("pe", e)

        @block.scalar
        def _(e):
            run("act", e)

        @block.vector
        def _(e):
            run("dve", e)

        @block.gpsimd
        def _(e):
            run("pool", e)

        @block.sync
        def _(e):
            run("sp", e)
            for k, v in outer.dmacnt.items():
                e.wait_ge(dmasem[k], v)


class Arena:
    def __init__(self, nc, cap=229376):
        self.nc = nc; self.off = 16512; self.cap = cap; self.n = 0

    def alloc(self, shape, dtype, name=None):
        size = int(np.prod(shape[1:])) * (4 if dtype == F32 else 2)
        size = (size + 63) // 64 * 64
        assert self.off + size <= self.cap, ("SBUF overflow", name, self.off, size)
        self.n += 1
        t = self.nc.alloc_sbuf_tensor_at("%s_%d" % (name or "t", self.n), list(shape), dtype, offset=self.off)
        self.off += size
        return t

    def mark(self):
        return self.off

    def release(self, m):
        self.off = m


def tiles_of(L, ts):
    out = []
    t = 0
    while t < L:
        out.append((t, min(ts, L - t)))
        t += ts
    return out


def build_program(LREAL, debug=False):
    L = LREAL + NM
    nc = bass.Bass("TRN2", target_bir_lowering=False)
    R = Rec()
    A = Arena(nc)
    dr = {}

    def din(name, shape, dt=F32):
        dr[name] = nc.dram_tensor(name, list(shape), dt, kind="ExternalInput").ap()
        return dr[name]

    def dscr(name, shape, dt, out=False):
        dr[name] = nc.dram_tensor(name, list(shape), dt, kind=("ExternalOutput" if out else "Internal")).ap()
        return dr[name]

    xT = din("xT", [D, LREAL]); metaT = din("metaT", [D, NM])
    cosT = din("cosT", [DR, L]); sinT = din("sinT", [DR, L]); trim = din("trim", [128, 128])
    g_mix = din("g_mix", [DEPTH, 128, DC]); g_ffn = din("g_ffn", [DEPTH, 128, DC]); g_fin = din("g_fin", [128, DC])
    g_q = din("g_q", [DEPTH, 128, 3]); g_kv = din("g_kv", [DEPTH, 128, 2])
    g_ao = din("g_ao", [DEPTH, 128, 4]); g_so = din("g_so", [DEPTH, 128, 4]); d_skip = din("d_skip", [DEPTH, 128, 4])
    w_in = din("w_in", [DEPTH, D, INW]); w_uq = din("w_uq", [DEPTH, QL, 768]); w_ukv = din("w_ukv", [DEPTH, KVL, 1024])
    w_glu = din("w_glu", [DEPTH, SW, SW]); w_o = din("w_o", [DEPTH, D, D])
    w_gate = din("w_gate", [DEPTH, D, FF]); w_up = din("w_up", [DEPTH, D, FF]); w_down = din("w_down", [DEPTH, FF, D])
    aR_re = din("aR_re", [DEPTH, 128, 4, 128]); aR_im = din("aR_im", [DEPTH, 128, 4, 128]); ldtR = din("ldtR", [DEPTH, 128, 4, 128])
    bR_re = din("bR_re", [DEPTH, 128, 16, 128]); bR_im = din("bR_im", [DEPTH, 128, 16, 128])
    aS_re = din("aS_re", [DEPTH, 128, 16]); aS_im = din("aS_im", [DEPTH, 128, 16]); ldtS = din("ldtS", [DEPTH, 128, 16])
    cS_re = din("cS_re", [DEPTH, 128, 16, 128]); cS_im = din("cS_im", [DEPTH, 128, 16, 128])

    hT = dscr("hT", [D, L], F32, out=debug)
    qnT = dscr("qnT", [H, 128, L], BF16, out=debug); qrT = dscr("qrT", [H, 64, L], BF16, out=debug)
    kT = dscr("kT", [H, 128, L], BF16, out=debug); krT = dscr("krT", [64, L], BF16, out=debug)
    vv = dscr("vv", [L, 512], BF16, out=debug); uT = dscr("uT", [SW, L], BF16, out=debug)
    ssmT = dscr("ssmT", [SW, L], BF16, out=debug); attT = dscr("attT", [SW, L], BF16, out=debug)
    outT = dscr("outT", [D, LREAL], F32, out=True)

    stack = ExitStack()
    psb = [nc.alloc_psum_tensor("psb%d" % i, [128, 512], F32) for i in range(8)]
    pcount = [0]

    def nextps(lo=0, hi=8):
        i = lo + pcount[0] % (hi - lo)
        pcount[0] += 1
        return psb[i], "ps%d" % i

    def mm(out, lhsT, rhs, start, stop, reads, writes):
        R.op("pe", lambda e, o=out, l=lhsT, r=rhs, s=start, t=stop: e.matmul(o, l, r, start=s, stop=t), reads, writes)

    def act(out, in_, func, reads, writes, scale=1.0, bias=0.0):
        R.op("act", lambda e, o=out, i=in_, f=func, s=scale, b=bias: e.activation(o, i, f, bias=b, scale=s), reads, writes)

    def tt(eng, out, in0, in1, op, reads, writes):
        R.op(eng, lambda e, o=out, a=in0, b=in1, p=op: e.tensor_tensor(o, a, b, p), reads, writes)

    def ts(eng, out, in0, s1, s2, op0, op1, reads, writes):
        if s2 is None:
            R.op(eng, lambda e, o=out, a=in0, x=s1, p=op0: e.tensor_scalar(o, a, x, None, p), reads, writes)
        else:
            R.op(eng, lambda e, o=out, a=in0, x=s1, y=s2, p=op0, q=op1: e.tensor_scalar(o, a, x, y, p, q), reads, writes)

    def cp(eng, out, in_, reads, writes):
        if eng == "act":
            R.op("act", lambda e, o=out, i=in_: e.copy(o, i), reads, writes)
        else:
            R.op(eng, lambda e, o=out, i=in_: e.tensor_copy(o, i), reads, writes)

    def recip(out, in_, reads, writes):
        R.op("dve", lambda e, o=out, i=in_: e.reciprocal(o, i), reads, writes)

    def memset(eng, ap, val, writes):
        R.op(eng, lambda e, a=ap, v=val: e.memset(a, v), (), writes)

    ones_bf = A.alloc([128, 128], BF16, "ones"); memset("pool", ones_bf[:], 1.0, ["ones"])
    tri32 = A.alloc([128, 128], F32, "tri32"); tri_bf = A.alloc([128, 128], BF16, "tri")
    R.dma("sp", tri32[:], trim[:, :], "c_tri", (), ["tri32"])
    cp("pool", tri_bf[:], tri32[:], ["tri32"], ["tri"])
    gfin = A.alloc([128, DC], F32, "gfin"); R.dma("sp", gfin[:], g_fin[:, :], "c_gfin", (), ["gfin"])
    stcnt = [0]
    stage = [None, None]

    def alloc_stage():
        for i in range(2):
            stage[i] = A.alloc([128, FF], F32, "stage%d" % i)

    def load_cast(wap, rows, cols, dst, gain, tag, extra=None):
        for kc in range(rows // 128):
            s = stcnt[0] % 2; stcnt[0] += 1
            sk = "stage%d" % s
            R.dma("sp", stage[s][:, 0:cols], wap[kc * 128:(kc + 1) * 128, :], "ld_" + sk, (), [sk])
            eng = ["dve", "pool"][kc % 2]
            if gain is None:
                cp(eng, dst[:, kc, :], stage[s][:, 0:cols], [sk], [tag])
            else:
                ts(eng, dst[:, kc, :], stage[s][:, 0:cols], gain[:, kc:kc + 1], None, ALU.mult, None, [sk, tag + "_g"], [tag])
            if extra is not None:
                extra(kc, stage[s], sk)

    def rms_rstd(sq_list, nfeat, n, tagw, rstd_ap, rkey):
        ps, pk = nextps()
        for i, (sap, sk) in enumerate(sq_list):
            mm(ps[:, 0:n], ones_bf[:, :], sap, i == 0, i == len(sq_list) - 1, ["ones", sk], [pk])
        act(rstd_ap, ps[:, 0:n], AF.Sqrt, [pk], [rkey], scale=1.0 / nfeat, bias=EPS)
        recip(rstd_ap, rstd_ap, [rkey], [rkey])

    R.dma("sp", hT[:, 0:NM], metaT[:, :], "init_h", (), ["hT"])
    R.dma("sp", hT[:, NM:L], xT[:, :], "init_h", (), ["hT"])
    R.barrier()
    base_mark = A.mark()
    tl = tiles_of(L, TS)

    for l in range(DEPTH):
        A.release(base_mark)
        gmix = A.alloc([128, DC], F32, "gmix"); gq = A.alloc([128, 3], F32, "gq"); gkv = A.alloc([128, 2], F32, "gkv")
        R.dma("sp", gmix[:], g_mix[l], "c_g1", (), ["win_g"])
        R.dma("sp", gq[:], g_q[l], "c_g2", (), ["wuq_g"])
        R.dma("sp", gkv[:], g_kv[l], "c_g3", (), ["wukv_g"])
        win_bf = A.alloc([128, DC, INW], BF16, "win"); win_rot = A.alloc([128, DC, 64], BF16, "winrot")
        wuq_bf = A.alloc([128, 3, 768], BF16, "wuq"); wuq_rot = A.alloc([128, 3, 256], BF16, "wuqrot")
        wukv_bf = A.alloc([128, 2, 1024], BF16, "wukv"); wukv_v = A.alloc([128, 2, 512], BF16, "wukvv")
        alloc_stage()

        def ex_in(kc, st, sk):
            ts("dve", win_rot[:, kc, 0:32], st[:, 672:704], gmix[:, kc:kc + 1], -1.0, ALU.mult, ALU.mult, [sk, "win_g"], ["winrot"])
            ts("pool", win_rot[:, kc, 32:64], st[:, 640:672], gmix[:, kc:kc + 1], None, ALU.mult, None, [sk, "win_g"], ["winrot"])
        load_cast(w_in[l], D, INW, win_bf, gmix, "win", ex_in)

        def ex_uq(kc, st, sk):
            for h in range(H):
                b = 192 * h + 128
                ts("dve", wuq_rot[:, kc, 64 * h:64 * h + 32], st[:, b + 32:b + 64], gq[:, kc:kc + 1], -1.0, ALU.mult, ALU.mult, [sk, "wuq_g"], ["wuqrot"])
                ts("pool", wuq_rot[:, kc, 64 * h + 32:64 * h + 64], st[:, b:b + 32], gq[:, kc:kc + 1], None, ALU.mult, None, [sk, "wuq_g"], ["wuqrot"])
        load_cast(w_uq[l], QL, 768, wuq_bf, gq, "wuq", ex_uq)

        def ex_kv(kc, st, sk):
            for h in range(H):
                ts("pool", wukv_v[:, kc, 128 * h:128 * h + 128], st[:, 256 * h + 128:256 * h + 256], gkv[:, kc:kc + 1], None, ALU.mult, None, [sk, "wukv_g"], ["wukvv"])
        load_cast(w_ukv[l], KVL, 1024, wukv_bf, gkv, "wukv", ex_kv)

        h32 = A.alloc([128, DC, TS], F32, "h32"); xb = A.alloc([128, DC, TS], BF16, "xb"); sq = A.alloc([128, DC, TS], BF16, "sq")
        rstd = A.alloc([128, TS], F32, "rstd"); rstdq = A.alloc([128, TS], F32, "rstdq"); rstdkv = A.alloc([128, TS], F32, "rstdkv")
        cos2 = A.alloc([64, TS], F32, "cos2"); sin2 = A.alloc([64, TS], F32, "sin2")
        cq_bf = A.alloc([128, 3, TS], BF16, "cq"); sqq = A.alloc([128, 3, TS], BF16, "sqq")
        ckv_bf = A.alloc([128, 2, TS], BF16, "ckv"); sqkv = A.alloc([128, 2, TS], BF16, "sqkv")
        u_bf = A.alloc([128, 4, TS], BF16, "ubf"); kr_bf = A.alloc([64, TS], BF16, "krbf")
        rp1 = A.alloc([64, TS], F32, "rp1"); rp2 = A.alloc([64, TS], F32, "rp2")
        qn_bf = A.alloc([128, H, TS], BF16, "qnbf"); qr_bf = A.alloc([64, H, TS], BF16, "qrbf")
        kn_bf = A.alloc([128, H, TS], BF16, "knbf"); v_bf = A.alloc([128, 4, 512], BF16, "vbf")
        rtok = A.alloc([128, 4], F32, "rtok")
        ones_col = ones_bf[:, 0:1]

        for (t0, n) in tl:
            R.dma("sp", h32[:, :, 0:n], hT[:, t0:t0 + n].rearrange("(c p) t -> p c t", p=128), "ld_h32", (), ["h32"])
            R.dma("sp", cos2[:, 0:n], cosT[:, t0:t0 + n], "ld_cs", (), ["cos2"])
            R.dma("sp", sin2[:, 0:n], sinT[:, t0:t0 + n], "ld_cs", (), ["sin2"])
            for c in range(DC):
                cp("pool", xb[:, c, 0:n], h32[:, c, 0:n], ["h32"], ["xb%d" % c])
                act(sq[:, c, 0:n], h32[:, c, 0:n], AF.Square, ["h32"], ["sq%d" % c])
            rms_rstd([(sq[:, c, 0:n], "sq%d" % c) for c in range(DC)], D, n, "A", rstd[:, 0:n], "rstd")

            def proj(col0, m, wbf=win_bf, wtag="win"):
                ps, pk = nextps()
                for c in range(DC):
                    mm(ps[0:m, 0:n], wbf[:, c, col0:col0 + m], xb[:, c, 0:n], c == 0, c == DC - 1, [wtag, "xb%d" % c], [pk])
                return ps, pk
            for m in range(3):
                ps, pk = proj(128 * m, 128)
                tt("dve", cq_bf[:, m, 0:n], ps[:, 0:n], rstd[:, 0:n], ALU.mult, [pk, "rstd"], ["cq%d" % m])
                act(sqq[:, m, 0:n], cq_bf[:, m, 0:n], AF.Square, ["cq%d" % m], ["sqq%d" % m])
            for m in range(2):
                ps, pk = proj(QL + 128 * m, 128)
                tt("dve", ckv_bf[:, m, 0:n], ps[:, 0:n], rstd[:, 0:n], ALU.mult, [pk, "rstd"], ["ckv%d" % m])
                act(sqkv[:, m, 0:n], ckv_bf[:, m, 0:n], AF.Square, ["ckv%d" % m], ["sqkv%d" % m])
            ps, pk = proj(QL + KVL, 64)
            ps2, pk2 = proj(0, 64, win_rot, "winrot")
            tt("dve", rp1[:, 0:n], ps[0:64, 0:n], cos2[:, 0:n], ALU.mult, [pk, "cos2"], ["rp1"])
            tt("dve", rp2[:, 0:n], ps2[0:64, 0:n], sin2[:, 0:n], ALU.mult, [pk2, "sin2"], ["rp2"])
            tt("pool", rp1[:, 0:n], rp1[:, 0:n], rp2[:, 0:n], ALU.add, ["rp1", "rp2"], ["rp1"])
            tt("pool", kr_bf[:, 0:n], rp1[:, 0:n], rstd[0:64, 0:n], ALU.mult, ["rp1", "rstd"], ["krbf"])
            R.dma("sp", krT[:, t0:t0 + n], kr_bf[:, 0:n], "st_kr", ["krbf"], ["krT"])
            for m in range(4):
                ps, pk = proj(QL + KVL + DR + 128 * m, 128)
                tt("dve", u_bf[:, m, 0:n], ps[:, 0:n], rstd[:, 0:n], ALU.mult, [pk, "rstd"], ["ubf"])
            R.dma("sp", uT[:, t0:t0 + n].rearrange("(c p) t -> p c t", p=128), u_bf[:, :, 0:n], "st_u", ["ubf"], ["uT"])
            rms_rstd([(sqq[:, m, 0:n], "sqq%d" % m) for m in range(3)], QL, n, "Aq", rstdq[:, 0:n], "rstdq")
            for h in range(H):
                ps, pk = nextps()
                for c in range(3):
                    mm(ps[:, 0:n], wuq_bf[:, c, 192 * h:192 * h + 128], cq_bf[:, c, 0:n], c == 0, c == 2, ["wuq", "cq%d" % c], [pk])
                tt("dve", qn_bf[:, h, 0:n], ps[:, 0:n], rstdq[:, 0:n], ALU.mult, [pk, "rstdq"], ["qnbf"])
                ps, pk = nextps()
                for c in range(3):
                    mm(ps[0:64, 0:n], wuq_bf[:, c, 192 * h + 128:192 * h + 192], cq_bf[:, c, 0:n], c == 0, c == 2, ["wuq", "cq%d" % c], [pk])
                ps2, pk2 = nextps()
                for c in range(3):
                    mm(ps2[0:64, 0:n], wuq_rot[:, c, 64 * h:64 * h + 64], cq_bf[:, c, 0:n], c == 0, c == 2, ["wuqrot", "cq%d" % c], [pk2])
                tt("dve", rp1[:, 0:n], ps[0:64, 0:n], cos2[:, 0:n], ALU.mult, [pk, "cos2"], ["rp1"])
                tt("dve", rp2[:, 0:n], ps2[0:64, 0:n], sin2[:, 0:n], ALU.mult, [pk2, "sin2"], ["rp2"])
                tt("pool", rp1[:, 0:n], rp1[:, 0:n], rp2[:, 0:n], ALU.add, ["rp1", "rp2"], ["rp1"])
                tt("pool", qr_bf[:, h, 0:n], rp1[:, 0:n], rstdq[0:64, 0:n], ALU.mult, ["rp1", "rstdq"], ["qrbf"])
            R.dma("sp", qnT[:, :, t0:t0 + n].rearrange("h p t -> p h t"), qn_bf[:, :, 0:n], "st_qn", ["qnbf"], ["qnT"])
            R.dma("sp", qrT[:, :, t0:t0 + n].rearrange("h p t -> p h t"), qr_bf[:, :, 0:n], "st_qr", ["qrbf"], ["qrT"])
            rms_rstd([(sqkv[:, m, 0:n], "sqkv%d" % m) for m in range(2)], KVL, n, "Akv", rstdkv[:, 0:n], "rstdkv")
            for h in range(H):
                ps, pk = nextps()
                for c in range(2):
                    mm(ps[:, 0:n], wukv_bf[:, c, 256 * h:256 * h + 128], ckv_bf[:, c, 0:n], c == 0, c == 1, ["wukv", "ckv%d" % c], [pk])
                tt("dve", kn_bf[:, h, 0:n], ps[:, 0:n], rstdkv[:, 0:n], ALU.mult, [pk, "rstdkv"], ["knbf"])
            R.dma("sp", kT[:, :, t0:t0 + n].rearrange("h p t -> p h t"), kn_bf[:, :, 0:n], "st_kn", ["knbf"], ["kT"])
            nsb = (n + 127) // 128
            for s in range(nsb):
                m = min(128, n - 128 * s)
                ps, pk = nextps()
                for c in range(2):
                    mm(ps[0:m, 0:1], sqkv[:, c, 128 * s:128 * s + m], ones_col, c == 0, c == 1, ["ones", "sqkv%d" % c], [pk])
                act(rtok[0:m, s:s + 1], ps[0:m, 0:1], AF.Sqrt, [pk], ["rtok%d" % s], scale=1.0 / KVL, bias=EPS)
                recip(rtok[0:m, s:s + 1], rtok[0:m, s:s + 1], ["rtok%d" % s], ["rtok%d" % s])
                ps, pk = nextps()
                for c in range(2):
                    mm(ps[0:m, 0:512], ckv_bf[:, c, 128 * s:128 * s + m], wukv_v[:, c, :], c == 0, c == 1, ["wukvv", "ckv%d" % c], [pk])
                ts("dve", v_bf[0:m, s, :], ps[0:m, 0:512], rtok[0:m, s:s + 1], None, ALU.mult, None, [pk, "rtok%d" % s], ["vbf%d" % s])
                R.dma("sp", vv[t0 + 128 * s:t0 + 128 * s + m, :], v_bf[0:m, s, :], "st_v%d" % s, ["vbf%d" % s], ["vv"])
        R.barrier()
        if debug and debug == "A%d" % l:
            break

        A.release(base_mark)
        def lam_prep(shape, a_re_d, a_im_d, ldt_d, pfx):
            t = {}
            for nm in ["lr", "li", "dt", "mag", "c", "s", "t1", "t2", "lre", "lim"]:
                t[nm] = A.alloc(shape, F32, pfx + nm)
            k = lambda nm: pfx + nm
            R.dma("sp", t["lr"][:], a_re_d, "c_" + pfx + "1", (), [k("lr")])
            R.dma("sp", t["li"][:], a_im_d, "c_" + pfx + "2", (), [k("li")])
            R.dma("sp", t["dt"][:], ldt_d, "c_" + pfx + "3", (), [k("dt")])
            ts("dve", t["lr"][:], t["lr"][:], -1e-4, None, ALU.min, None, [k("lr")], [k("lr")])
            act(t["dt"][:], t["dt"][:], AF.Exp, [k("dt")], [k("dt")])
            tt("dve", t["t1"][:], t["lr"][:], t["dt"][:], ALU.mult, [k("lr"), k("dt")], [k("t1")])
            act(t["mag"][:], t["t1"][:], AF.Exp, [k("t1")], [k("mag")])
            tt("dve", t["t2"][:], t["li"][:], t["dt"][:], ALU.mult, [k("li"), k("dt")], [k("t2")])
            act(t["s"][:], t["t2"][:], AF.Sin, [k("t2")], [k("s")], scale=1.0 / 32.0)
            act(t["c"][:], t["t2"][:], AF.Sin, [k("t2")], [k("c")], scale=1.0 / 32.0, bias=math.pi / 2.0)
            for _ in range(5):
                tt("dve", t["t1"][:], t["c"][:], t["c"][:], ALU.mult, [k("c")], [k("t1")])
                tt("dve", t["t2"][:], t["s"][:], t["s"][:], ALU.mult, [k("s")], [k("t2")])
                tt("dve", t["s"][:], t["c"][:], t["s"][:], ALU.mult, [k("c"), k("s")], [k("s")])
                ts("dve", t["s"][:], t["s"][:], 2.0, None, ALU.mult, None, [k("s")], [k("s")])
                tt("dve", t["c"][:], t["t1"][:], t["t2"][:], ALU.subtract, [k("t1"), k("t2")], [k("c")])
            tt("dve", t["lre"][:], t["mag"][:], t["c"][:], ALU.mult, [k("mag"), k("c")], [k("lre")])
            tt("dve", t["lim"][:], t["mag"][:], t["s"][:], ALU.mult, [k("mag"), k("s")], [k("lim")])
            return t

        pm = A.mark()
        LBre = A.alloc([128, 16, 128], BF16, "LBre"); LBim = A.alloc([128, 16, 128], BF16, "LBim")
        LCre = A.alloc([128, 16, 128], BF16, "LCre"); LCim = A.alloc([128, 16, 128], BF16, "LCim")
        diagD = A.alloc([128, 4, 128], BF16, "diagD")
        Ere = A.alloc([128, 16, TS], BF16, "Ere"); Eim = A.alloc([128, 16, TS], BF16, "Eim")
        Elre = A.alloc([128, 16], F32, "Elre"); Elim = A.alloc([128, 16], F32, "Elim")
        rdec = A.alloc([128, 16], F32, "rdec")
        car_re = A.alloc([128, 16], F32, "carre"); car_im = A.alloc([128, 16], F32, "carim")
        wglu_bf = A.alloc([128, 4, SW], BF16, "wglu")
        dsk = A.alloc([128, 4], F32, "dsk")
        idt = A.alloc([128, 128], F32, "idt")
        pm2 = A.mark()
        tR = lam_prep([128, 4, 128], aR_re[l], aR_im[l], ldtR[l], "R")
        kR = lambda nm: "R" + nm
        den = A.alloc([128, 4, 128], F32, "den"); cre = A.alloc([128, 4, 128], F32, "cre"); cim = A.alloc([128, 4, 128], F32, "cim")
        b32r = A.alloc([128, 16, 128], F32, "b32r"); b32i = A.alloc([128, 16, 128], F32, "b32i")
        w1 = A.alloc([128, 16, 128], F32, "w1"); w2 = A.alloc([128, 16, 128], F32, "w2")
        R.dma("sp", b32r[:], bR_re[l], "c_b1", (), ["b32r"])
        R.dma("sp", b32i[:], bR_im[l], "c_b2", (), ["b32i"])
        tt("dve", den[:], tR["lr"][:], tR["lr"][:], ALU.mult, [kR("lr")], ["den"])
        tt("dve", tR["t1"][:], tR["li"][:], tR["li"][:], ALU.mult, [kR("li")], [kR("t1")])
        tt("dve", den[:], den[:], tR["t1"][:], ALU.add, ["den", kR("t1")], ["den"])
        recip(den[:], den[:], ["den"], ["den"])
        ts("dve", tR["lre"][:], tR["lre"][:], -1.0, None, ALU.add, None, [kR("lre")], [kR("lre")])
        tt("dve", tR["t1"][:], tR["lre"][:], tR["lr"][:], ALU.mult, [kR("lre"), kR("lr")], [kR("t1")])
        tt("dve", tR["t2"][:], tR["lim"][:], tR["li"][:], ALU.mult, [kR("lim"), kR("li")], [kR("t2")])
        tt("dve", tR["t1"][:], tR["t1"][:], tR["t2"][:], ALU.add, [kR("t1"), kR("t2")], [kR("t1")])
        tt("dve", cre[:], tR["t1"][:], den[:], ALU.mult, [kR("t1"), "den"], ["cre"])
        tt("dve", tR["t1"][:], tR["lim"][:], tR["lr"][:], ALU.mult, [kR("lim"), kR("lr")], [kR("t1")])
        tt("dve", tR["t2"][:], tR["lre"][:], tR["li"][:], ALU.mult, [kR("lre"), kR("li")], [kR("t2")])
        tt("dve", tR["t1"][:], tR["t1"][:], tR["t2"][:], ALU.subtract, [kR("t1"), kR("t2")], [kR("t1")])
        tt("dve", cim[:], tR["t1"][:], den[:], ALU.mult, [kR("t1"), "den"], ["cim"])
        for c in range(4):
            creb = cre[:, c:c + 1, :].to_broadcast([128, 4, 128]); cimb = cim[:, c:c + 1, :].to_broadcast([128, 4, 128])
            sl = slice(4 * c, 4 * c + 4)
            tt("dve", w1[:, sl, :], b32r[:, sl, :], creb, ALU.mult, ["b32r", "cre"], ["w1"])
            tt("dve", w2[:, sl, :], b32i[:, sl, :], cimb, ALU.mult, ["b32i", "cim"], ["w2"])
            tt("dve", LBre[:, sl, :], w1[:, sl, :], w2[:, sl, :], ALU.subtract, ["w1", "w2"], ["LBre"])
            tt("dve", w1[:, sl, :], b32i[:, sl, :], creb, ALU.mult, ["b32i", "cre"], ["w1"])
            tt("dve", w2[:, sl, :], b32r[:, sl, :], cimb, ALU.mult, ["b32r", "cim"], ["w2"])
            tt("dve", LBim[:, sl, :], w1[:, sl, :], w2[:, sl, :], ALU.add, ["w1", "w2"], ["LBim"])
        R.dma("sp", b32r[:], cS_re[l], "c_b1", (), ["b32r"])
        R.dma("sp", b32i[:], cS_im[l], "c_b2", (), ["b32i"])
        cp("pool", LCre[:], b32r[:], ["b32r"], ["LCre"])
        ts("pool", LCim[:], b32i[:], -1.0, None, ALU.mult, None, ["b32i"], ["LCim"])
        R.dma("sp", dsk[:], d_skip[l], "c_dsk", (), ["dsk"])
        R.barrier()
        A.release(pm2)
        tS = lam_prep([128, 16], aS_re[l], aS_im[l], ldtS[l], "S")
        kS = lambda nm: "S" + nm
        cp("dve", rdec[:], tS["mag"][:], [kS("mag")], ["rdec"])
        E32r = A.alloc([128, 16, TS], F32, "E32r"); E32i = A.alloc([128, 16, TS], F32, "E32i")
        tmpa = A.alloc([128, 16, TS // 2], F32, "tmpa"); tmpb = A.alloc([128, 16, TS // 2], F32, "tmpb")
        cp("dve", E32r[:, :, 0:1], tS["c"][:].unsqueeze(2), [kS("c")], ["E32r"])
        cp("dve", E32i[:, :, 0:1], tS["s"][:].unsqueeze(2), [kS("s")], ["E32i"])
        nn = 1
        while nn < TS:
            crb = E32r[:, :, nn - 1:nn].to_broadcast([128, 16, nn]); cib = E32i[:, :, nn - 1:nn].to_broadcast([128, 16, nn])
            tt("dve", tmpa[:, :, 0:nn], E32r[:, :, 0:nn], crb, ALU.mult, ["E32r"], ["tmpa"])
            tt("dve", tmpb[:, :, 0:nn], E32i[:, :, 0:nn], cib, ALU.mult, ["E32i"], ["tmpb"])
            tt("dve", tmpa[:, :, 0:nn], tmpa[:, :, 0:nn], tmpb[:, :, 0:nn], ALU.subtract, ["tmpa", "tmpb"], ["tmpa"])
            tt("dve", tmpb[:, :, 0:nn], E32r[:, :, 0:nn], cib, ALU.mult, ["E32r", "E32i"], ["tmpb"])
            cp("pool", E32r[:, :, nn:2 * nn], tmpa[:, :, 0:nn], ["tmpa"], ["E32r"])
            tt("dve", tmpa[:, :, 0:nn], E32i[:, :, 0:nn], crb, ALU.mult, ["E32i", "E32r"], ["tmpa"])
            tt("dve", E32i[:, :, nn:2 * nn], tmpa[:, :, 0:nn], tmpb[:, :, 0:nn], ALU.add, ["tmpa", "tmpb"], ["E32i"])
            nn *= 2
        cp("pool", Ere[:], E32r[:], ["E32r"], ["Ere"])
        cp("pool", Eim[:], E32i[:], ["E32i"], ["Eim"])
        cp("dve", Elre[:], E32r[:, :, TS - 1], ["E32r"], ["Elre"])
        cp("dve", Elim[:], E32i[:, :, TS - 1], ["E32i"], ["Elim"])
        memset("pool", car_re[:], 0.0, ["carre%d" % q for q in range(16)]); memset("pool", car_im[:], 0.0, ["carim%d" % q for q in range(16)])
        R.barrier()
        A.release(pm2)
        alloc_stage()
        load_cast(w_glu[l], SW, SW, wglu_bf, None, "wglu")
        memset("pool", idt[:], 0.0, ["idt"])
        cp("pool", idt[:, :], tri32[:, :], ["tri32", "idt"], ["idt"])
        tt("pool", idt[:, 1:128], tri32[:, 1:128], tri32[:, 0:127], ALU.subtract, ["tri32", "idt"], ["idt"])
        for c in range(4):
            ts("dve", diagD[:, c, :], idt[:, :], dsk[:, c:c + 1], None, ALU.mult, None, ["idt", "dsk"], ["diagD"])
        R.barrier()
        A.release(pm2)
        ub = A.alloc([128, 4, TS], BF16, "ub")
        m1 = A.alloc([128, TS], BF16, "m1"); m2 = A.alloc([128, TS], BF16, "m2")
        wre = [A.alloc([128, TS], BF16, "wre%d" % i) for i in range(2)]
        wim = [A.alloc([128, TS], BF16, "wim%d" % i) for i in range(2)]
        rb = A.alloc([128, TS], F32, "rb")
        z32r = [A.alloc([128, TS], F32, "z32r%d" % i) for i in range(2)]
        z32i = [A.alloc([128, TS], F32, "z32i%d" % i) for i in range(2)]
        zbr = [A.alloc([128, TS], BF16, "zbr%d" % i) for i in range(2)]
        zbi = [A.alloc([128, TS], BF16, "zbi%d" % i) for i in range(2)]
        xre = [A.alloc([128, TS], BF16, "xre%d" % i) for i in range(2)]
        xim = [A.alloc([128, TS], BF16, "xim%d" % i) for i in range(2)]
        cc = A.alloc([128, 8], F32, "cc")
        y32 = A.alloc([128, TS], F32, "y32"); y2 = A.alloc([128, TS], F32, "y2"); sg = A.alloc([128, TS], F32, "sg")
        g32 = A.alloc([128, 4, TS], F32, "g32"); g_bf = A.alloc([128, 4, TS], BF16, "gbf")
        s32 = A.alloc([128, 4, TS], F32, "s32"); sqs = A.alloc([128, 4, TS], BF16, "sqs")
        rstds = A.alloc([128, TS], F32, "rstds"); sn_bf = A.alloc([128, 4, TS], BF16, "snbf")

        for (t0, n) in tl:
            R.dma("sp", ub[:, :, 0:n], uT[:, t0:t0 + n].rearrange("(c p) t -> p c t", p=128), "ld_ub", (), ["ub"])
            for c in range(4):
                psy, pky = psb[6 + (c % 2)], "ps%d" % (6 + (c % 2))
                mm(psy[:, 0:n], diagD[:, c, :], ub[:, c, 0:n], True, False, ["diagD", "ub"], [pky])
                for j in range(4):
                    q = 4 * c + j
                    s = q % 2
                    sfx = "%d" % s
                    pr, pkr = nextps(0, 6); pi, pki = nextps(0, 6)
                    mm(pr[:, 0:n], LBre[:, q, :], ub[:, c, 0:n], True, True, ["LBre", "ub"], [pkr])
                    mm(pi[:, 0:n], LBim[:, q, :], ub[:, c, 0:n], True, True, ["LBim", "ub"], [pki])
                    tt("dve", m1[:, 0:n], pr[:, 0:n], Ere[:, q, 0:n], ALU.mult, [pkr, "Ere"], ["m1"])
                    tt("dve", m2[:, 0:n], pi[:, 0:n], Eim[:, q, 0:n], ALU.mult, [pki, "Eim"], ["m2"])
                    tt("dve", wre[s][:, 0:n], m1[:, 0:n], m2[:, 0:n], ALU.add, ["m1", "m2"], ["wre" + sfx])
                    tt("dve", m1[:, 0:n], pi[:, 0:n], Ere[:, q, 0:n], ALU.mult, [pki, "Ere"], ["m1"])
                    tt("dve", m2[:, 0:n], pr[:, 0:n], Eim[:, q, 0:n], ALU.mult, [pkr, "Eim"], ["m2"])
                    tt("dve", wim[s][:, 0:n], m1[:, 0:n], m2[:, 0:n], ALU.subtract, ["m1", "m2"], ["wim" + sfx])
                    cp("pool", rb[:, 0:n], rdec[:, q:q + 1].to_broadcast([128, n]), ["rdec"], ["rb"])
                    R.op("dve", lambda e, o=z32r[s][:, 0:n], a=rb[:, 0:n], b=wre[s][:, 0:n], i=car_re[:, q:q + 1]:
                         e.tensor_tensor_scan(o, a, b, i, ALU.mult, ALU.add), ["rb", "wre" + sfx, "carre%d" % q], ["z32r" + sfx])
                    R.op("dve", lambda e, o=z32i[s][:, 0:n], a=rb[:, 0:n], b=wim[s][:, 0:n], i=car_im[:, q:q + 1]:
                         e.tensor_tensor_scan(o, a, b, i, ALU.mult, ALU.add), ["rb", "wim" + sfx, "carim%d" % q], ["z32i" + sfx])
                    cp("pool", zbr[s][:, 0:n], z32r[s][:, 0:n], ["z32r" + sfx], ["zbr" + sfx])
                    cp("pool", zbi[s][:, 0:n], z32i[s][:, 0:n], ["z32i" + sfx], ["zbi" + sfx])
                    if n == TS:
                        zr = z32r[s][:, n - 1:n]; zi = z32i[s][:, n - 1:n]
                        ts("pool", cc[:, 0:1], zr, Elre[:, q:q + 1], None, ALU.mult, None, ["z32r" + sfx, "Elre"], ["cc0"])
                        ts("pool", cc[:, 1:2], zi, Elim[:, q:q + 1], None, ALU.mult, None, ["z32i" + sfx, "Elim"], ["cc1"])
                        ts("pool", cc[:, 2:3], zr, Elim[:, q:q + 1], None, ALU.mult, None, ["z32r" + sfx, "Elim"], ["cc2"])
                        ts("pool", cc[:, 3:4], zi, Elre[:, q:q + 1], None, ALU.mult, None, ["z32i" + sfx, "Elre"], ["cc3"])
                        tt("pool", car_re[:, q:q + 1], cc[:, 0:1], cc[:, 1:2], ALU.subtract, ["cc0", "cc1"], ["carre%d" % q])
                        tt("pool", car_im[:, q:q + 1], cc[:, 2:3], cc[:, 3:4], ALU.add, ["cc2", "cc3"], ["carim%d" % q])
                    tt("dve", m1[:, 0:n], zbr[s][:, 0:n], Ere[:, q, 0:n], ALU.mult, ["zbr" + sfx, "Ere"], ["m1"])
                    tt("dve", m2[:, 0:n], zbi[s][:, 0:n], Eim[:, q, 0:n], ALU.mult, ["zbi" + sfx, "Eim"], ["m2"])
                    tt("dve", xre[s][:, 0:n], m1[:, 0:n], m2[:, 0:n], ALU.subtract, ["m1", "m2"], ["xre" + sfx])
                    tt("dve", m1[:, 0:n], zbr[s][:, 0:n], Eim[:, q, 0:n], ALU.mult, ["zbr" + sfx, "Eim"], ["m1"])
                    tt("dve", m2[:, 0:n], zbi[s][:, 0:n], Ere[:, q, 0:n], ALU.mult, ["zbi" + sfx, "Ere"], ["m2"])
                    tt("dve", xim[s][:, 0:n], m1[:, 0:n], m2[:, 0:n], ALU.add, ["m1", "m2"], ["xim" + sfx])
                    mm(psy[:, 0:n], LCre[:, q, :], xre[s][:, 0:n], False, False, ["LCre", "xre" + sfx], [pky])
                    mm(psy[:, 0:n], LCim[:, q, :], xim[s][:, 0:n], False, j == 3, ["LCim", "xim" + sfx], [pky])
                act(y32[:, 0:n], psy[:, 0:n], AF.Identity, [pky], ["y32"])
                act(y2[:, 0:n], psy[:, 0:n], AF.Square, [pky], ["y2"])
                ts("pool", y2[:, 0:n], y2[:, 0:n], 0.044715, 1.0, ALU.mult, ALU.add, ["y2"], ["y2"])
                tt("pool", y2[:, 0:n], y2[:, 0:n], y32[:, 0:n], ALU.mult, ["y2", "y32"], ["y2"])
                act(sg[:, 0:n], y2[:, 0:n], AF.Sigmoid, ["y2"], ["sg"], scale=2.0 * math.sqrt(2.0 / math.pi))
                tt("pool", g32[:, c, 0:n], y32[:, 0:n], sg[:, 0:n], ALU.mult, ["y32", "sg"], ["g32_%d" % c])
                cp("pool", g_bf[:, c, 0:n], g32[:, c, 0:n], ["g32_%d" % c], ["gbf%d" % c])
            for oc in range(4):
                ps, pk = nextps(0, 6)
                for kc in range(4):
                    mm(ps[:, 0:n], wglu_bf[:, kc, 128 * oc:128 * oc + 128], g_bf[:, kc, 0:n], kc == 0, kc == 3, ["wglu", "gbf%d" % kc], [pk])
                act(sg[:, 0:n], ps[:, 0:n], AF.Sigmoid, [pk], ["sg"])
                tt("pool", s32[:, oc, 0:n], g32[:, oc, 0:n], sg[:, 0:n], ALU.mult, ["g32_%d" % oc, "sg"], ["s32_%d" % oc])
                act(sqs[:, oc, 0:n], s32[:, oc, 0:n], AF.Square, ["s32_%d" % oc], ["sqs%d" % oc])
            rms_rstd([(sqs[:, oc, 0:n], "sqs%d" % oc) for oc in range(4)], SW, n, "Bs", rstds[:, 0:n], "rstds")
            for oc in range(4):
                tt("pool", sn_bf[:, oc, 0:n], s32[:, oc, 0:n], rstds[:, 0:n], ALU.mult, ["s32_%d" % oc, "rstds"], ["snbf"])
            R.dma("sp", ssmT[:, t0:t0 + n].rearrange("(c p) t -> p c t", p=128), sn_bf[:, :, 0:n], "st_sn", ["snbf"], ["ssmT"])
        R.barrier()
        if debug and debug == "B%d" % l:
            break

        A.release(base_mark)
        NKB = (L + 127) // 128
        kres = A.alloc([128, H, L], BF16, "kres"); krres = A.alloc([64, L], BF16, "krres")
        vres = A.alloc([128, NKB, 512], BF16, "vres")
        for h in range(H):
            R.dma("sp", kres[:, h, :], kT[h], "ld_kres", (), ["kres"])
        R.dma("sp", krres[:, :], krT[:, :], "ld_kres", (), ["kres"])
        nfull = L // 128
        R.dma("sp", vres[:, 0:nfull, :], vv[0:nfull * 128, :].rearrange("(b p) d -> p b d", p=128), "ld_kres", (), ["kres"])
        if L % 128:
            R.dma("sp", vres[0:L % 128, nfull, :], vv[nfull * 128:L, :], "ld_kres", (), ["kres"])
        qn = A.alloc([128, H, TS], BF16, "qn"); qr = A.alloc([64, H, TS], BF16, "qr")
        pT = [A.alloc([128, TS], BF16, "pT%d" % i) for i in range(3)]
        rinv = A.alloc([128, TS], F32, "rinv")
        at32 = A.alloc([128, H, TS], F32, "at32"); sqa = A.alloc([128, H, TS], BF16, "sqa")
        rstda = A.alloc([128, TS], F32, "rstda"); an_bf = A.alloc([128, H, TS], BF16, "anbf")
        pcnt = 0
        for (t0, n) in tl:
            R.dma("sp", qn[:, :, 0:n], qnT[:, :, t0:t0 + n].rearrange("h p t -> p h t"), "ld_qn", (), ["qn"])
            R.dma("sp", qr[:, :, 0:n], qrT[:, :, t0:t0 + n].rearrange("h p t -> p h t"), "ld_qr", (), ["qr"])
            kbmax = (t0 + n - 1) // 128
            for h in range(H):
                po, pko = psb[4 + (h % 2)], "ps%d" % (4 + (h % 2))
                prr, pkrr = psb[6 + (h % 2)], "ps%d" % (6 + (h % 2))
                for kb in range(kbmax + 1):
                    ks = min(128, L - 128 * kb)
                    c0 = max(0, 128 * kb - t0)
                    sb = pcnt % 3; pcnt += 1
                    pss, pks = psb[sb], "ps%d" % sb
                    mm(pss[0:ks, c0:n], kres[:, h, 128 * kb:128 * kb + ks], qn[:, h, c0:n], True, False, ["kres", "qn"], [pks])
                    mm(pss[0:ks, c0:n], krres[:, 128 * kb:128 * kb + ks], qr[:, h, c0:n], False, True, ["kres", "qr"], [pks])
                    pk_ = "pT%d" % sb
                    act(pT[sb][0:ks, c0:n], pss[0:ks, c0:n], AF.Exp, [pks], [pk_], scale=SCALE)
                    if 128 * kb >= t0:
                        w = min(128, n - c0)
                        tt("pool", pT[sb][0:ks, c0:c0 + w], pT[sb][0:ks, c0:c0 + w], tri_bf[0:ks, 0:w], ALU.mult, [pk_, "tri"], [pk_])
                    last = kb == kbmax
                    mm(po[:, c0:n], vres[0:ks, kb, 128 * h:128 * h + 128], pT[sb][0:ks, c0:n], kb == 0, last, ["kres", pk_], [pko])
                    mm(prr[:, c0:n], ones_bf[0:ks, :], pT[sb][0:ks, c0:n], kb == 0, last, ["ones", pk_], [pkrr])
                recip(rinv[:, 0:n], prr[:, 0:n], [pkrr], ["rinv"])
                tt("dve", at32[:, h, 0:n], po[:, 0:n], rinv[:, 0:n], ALU.mult, [pko, "rinv"], ["at32_%d" % h])
                act(sqa[:, h, 0:n], at32[:, h, 0:n], AF.Square, ["at32_%d" % h], ["sqa%d" % h])
            ps, pk = psb[3], "ps3"
            for h in range(H):
                mm(ps[:, 0:n], ones_bf[:, :], sqa[:, h, 0:n], h == 0, h == H - 1, ["ones", "sqa%d" % h], [pk])
            act(rstda[:, 0:n], ps[:, 0:n], AF.Sqrt, [pk], ["rstda"], scale=1.0 / SW, bias=EPS)
            recip(rstda[:, 0:n], rstda[:, 0:n], ["rstda"], ["rstda"])
            for h in range(H):
                tt("pool", an_bf[:, h, 0:n], at32[:, h, 0:n], rstda[:, 0:n], ALU.mult, ["at32_%d" % h, "rstda"], ["anbf"])
            R.dma("sp", attT[:, t0:t0 + n].rearrange("(c p) t -> p c t", p=128), an_bf[:, :, 0:n], "st_an", ["anbf"], ["attT"])
        R.barrier()
        if debug and debug == "C%d" % l:
            break

        A.release(base_mark)
        T2 = 256
        gao = A.alloc([128, 8], F32, "gao")
        R.dma("sp", gao[:, 0:4], g_ao[l], "c_gao", (), ["wo_g"])
        R.dma("sp", gao[:, 4:8], g_so[l], "c_gao2", (), ["wo_g"])
        gff = A.alloc([128, DC], F32, "gff"); R.dma("sp", gff[:], g_ffn[l], "c_gff", (), ["wg_g", "wu_g"])
        wo_bf = A.alloc([128, 8, D], BF16, "wo"); wg_bf = A.alloc([128, DC, FF], BF16, "wg")
        wu_bf = A.alloc([128, DC, FF], BF16, "wu"); wd_bf = A.alloc([128, FC, D], BF16, "wd")
        pm3 = A.mark()
        alloc_stage()
        load_cast(w_o[l], D, D, wo_bf, gao, "wo")
        load_cast(w_gate[l], D, FF, wg_bf, gff, "wg")
        load_cast(w_up[l], D, FF, wu_bf, gff, "wu")
        load_cast(w_down[l], FF, D, wd_bf, None, "wd")
        R.barrier()
        A.release(pm3)
        hh = A.alloc([128, DC, T2], F32, "hh"); xb2 = A.alloc([128, DC, T2], BF16, "xb2"); sq2 = A.alloc([128, DC, T2], BF16, "sq2")
        mix = A.alloc([128, 8, T2], BF16, "mix"); rstdf = A.alloc([128, T2], F32, "rstdf")
        gs = A.alloc([128, T2], F32, "gs"); us = A.alloc([128, T2], F32, "us"); sl_ = A.alloc([128, T2], F32, "sl")
        a_bf = A.alloc([128, FC, T2], BF16, "abf"); o32 = hh
        for (t0, n) in tiles_of(L, T2):
            R.dma("sp", hh[:, :, 0:n], hT[:, t0:t0 + n].rearrange("(c p) t -> p c t", p=128), "ld_hh", (), ["hh"])
            R.dma("sp", mix[:, 0:4, 0:n], attT[:, t0:t0 + n].rearrange("(c p) t -> p c t", p=128), "ld_mix", (), ["mixa"])
            R.dma("sp", mix[:, 4:8, 0:n], ssmT[:, t0:t0 + n].rearrange("(c p) t -> p c t", p=128), "ld_mix2", (), ["mixs"])
            for oc in range(DC):
                ps, pk = nextps()
                for kc in range(8):
                    mm(ps[:, 0:n], wo_bf[:, kc, 128 * oc:128 * oc + 128], mix[:, kc, 0:n], kc == 0, kc == 7, ["wo", "mixa", "mixs"], [pk])
                tt("dve", hh[:, oc, 0:n], ps[:, 0:n], hh[:, oc, 0:n], ALU.add, [pk, "hh"], ["hh%d" % oc])
                cp("pool", xb2[:, oc, 0:n], hh[:, oc, 0:n], ["hh%d" % oc], ["xb2_%d" % oc])
                act(sq2[:, oc, 0:n], hh[:, oc, 0:n], AF.Square, ["hh%d" % oc], ["sq2_%d" % oc])
            rms_rstd([(sq2[:, c, 0:n], "sq2_%d" % c) for c in range(DC)], D, n, "F", rstdf[:, 0:n], "rstdf")
            for fc in range(FC):
                pg, pkg = nextps(); pu, pku = nextps()
                for c in range(DC):
                    mm(pg[:, 0:n], wg_bf[:, c, 128 * fc:128 * fc + 128], xb2[:, c, 0:n], c == 0, c == DC - 1, ["wg", "xb2_%d" % c], [pkg])
                for c in range(DC):
                    mm(pu[:, 0:n], wu_bf[:, c, 128 * fc:128 * fc + 128], xb2[:, c, 0:n], c == 0, c == DC - 1, ["wu", "xb2_%d" % c], [pku])
                tt("dve", gs[:, 0:n], pg[:, 0:n], rstdf[:, 0:n], ALU.mult, [pkg, "rstdf"], ["gs"])
                tt("dve", us[:, 0:n], pu[:, 0:n], rstdf[:, 0:n], ALU.mult, [pku, "rstdf"], ["us"])
                act(sl_[:, 0:n], gs[:, 0:n], AF.Silu, ["gs"], ["sl"])
                tt("pool", a_bf[:, fc, 0:n], sl_[:, 0:n], us[:, 0:n], ALU.mult, ["sl", "us"], ["abf%d" % fc])
            last_layer = (l == DEPTH - 1)
            for oc in range(DC):
                ps, pk = nextps()
                for fc in range(FC):
                    mm(ps[:, 0:n], wd_bf[:, fc, 128 * oc:128 * oc + 128], a_bf[:, fc, 0:n], fc == 0, fc == FC - 1, ["wd", "abf%d" % fc], [pk])
                tt("dve", hh[:, oc, 0:n], ps[:, 0:n], hh[:, oc, 0:n], ALU.add, [pk, "hh%d" % oc], ["hh%d" % oc])
                if last_layer:
                    act(sq2[:, oc, 0:n], hh[:, oc, 0:n], AF.Square, ["hh%d" % oc], ["sq2_%d" % oc])
            if not last_layer:
                R.dma("sp", hT[:, t0:t0 + n].rearrange("(c p) t -> p c t", p=128), hh[:, :, 0:n], "st_hh",
                      ["hh"] + ["hh%d" % oc for oc in range(DC)], ["hT"])
            else:
                rms_rstd([(sq2[:, c, 0:n], "sq2_%d" % c) for c in range(DC)], D, n, "G", rstdf[:, 0:n], "rstdf")
                for oc in range(DC):
                    R.op("dve", lambda e, o=o32[:, oc, 0:n], a=hh[:, oc, 0:n], s=gfin[:, oc:oc + 1], b=rstdf[:, 0:n]:
                         e.scalar_tensor_tensor(o, a, s, b, ALU.mult, ALU.mult), ["hh%d" % oc, "rstdf", "gfin"], ["hh%d" % oc])
                a0 = max(t0, NM)
                if t0 + n > a0:
                    R.dma("sp", outT[:, a0 - NM:t0 + n - NM].rearrange("(c p) t -> p c t", p=128), o32[:, :, a0 - t0:n], "st_o",
                          ["hh"] + ["hh%d" % oc for oc in range(DC)], ["outT"])
        R.barrier()

    R.emit(nc, stack)
    stack.close()
    return nc


def _lay_vec(v, C):
    return np.ascontiguousarray(v.reshape(C, 128).T)


def host_layout(inp, LREAL):
    L = LREAL + NM
    f = np.float32
    out = {}
    out["metaT"] = np.ascontiguousarray(inp["meta_tokens"].T.astype(f))
    pos = np.arange(L, dtype=f)
    inv = (1.0 / (np.float32(10000.0) ** (np.arange(0, DR, 2, dtype=f) / np.float32(DR)))).astype(f)
    ang = (pos[:, None] * inv[None, :]).astype(f)
    c = np.cos(ang).astype(f).T; s = np.sin(ang).astype(f).T
    out["cosT"] = np.ascontiguousarray(np.concatenate([c, c], 0)); out["sinT"] = np.ascontiguousarray(np.concatenate([s, s], 0))
    k = np.arange(128)
    out["trim"] = (k[None, :] >= k[:, None]).astype(f)
    out["g_mix"] = np.stack([_lay_vec(inp["norm_mix_g"][l], DC) for l in range(DEPTH)])
    out["g_ffn"] = np.stack([_lay_vec(inp["norm_ffn_g"][l], DC) for l in range(DEPTH)])
    out["g_fin"] = _lay_vec(inp["final_norm_g"], DC)
    out["g_q"] = np.stack([_lay_vec(inp["q_norm_g"][l], 3) for l in range(DEPTH)])
    out["g_kv"] = np.stack([_lay_vec(inp["kv_norm_g"][l], 2) for l in range(DEPTH)])
    out["g_ao"] = np.stack([_lay_vec(inp["attn_out_g"][l], 4) for l in range(DEPTH)])
    out["g_so"] = np.stack([_lay_vec(inp["ssm_out_g"][l], 4) for l in range(DEPTH)])
    out["d_skip"] = np.stack([_lay_vec(inp["ssm_d"][l], 4) for l in range(DEPTH)])
    for nm in ["w_in", "w_uq", "w_ukv", "w_glu", "w_o", "w_gate", "w_up", "w_down"]:
        out[nm] = np.ascontiguousarray(inp[nm].astype(f))
    aR_re = np.zeros((DEPTH, 128, 4, 128), f); aR_im = np.zeros_like(aR_re); ldtR = np.zeros_like(aR_re)
    bR_re = np.zeros((DEPTH, 128, 16, 128), f); bR_im = np.zeros_like(bR_re)
    aS_re = np.zeros((DEPTH, 128, 16), f); aS_im = np.zeros_like(aS_re); ldtS = np.zeros_like(aS_re)
    cS_re = np.zeros((DEPTH, 128, 16, 128), f); cS_im = np.zeros_like(cS_re)
    for l in range(DEPTH):
        are, aim, ldt = inp["ssm_a_re"][l], inp["ssm_a_im"][l], inp["ssm_log_dt"][l]
        bre, bim, cre, cim = inp["ssm_b_re"][l], inp["ssm_b_im"][l], inp["ssm_c_re"][l], inp["ssm_c_im"][l]
        for q in range(16):
            for gl in range(2):
                g = 2 * q + gl
                aS_re[l, 64 * gl:64 * gl + 64, q] = are[g]; aS_im[l, 64 * gl:64 * gl + 64, q] = aim[g]
                ldtS[l, 64 * gl:64 * gl + 64, q] = ldt[g]
                j = q % 4
                cS_re[l, 64 * gl:64 * gl + 64, q, 32 * j + 16 * gl:32 * j + 16 * gl + 16] = cre[g].T
                cS_im[l, 64 * gl:64 * gl + 64, q, 32 * j + 16 * gl:32 * j + 16 * gl + 16] = cim[g].T
                bR_re[l, 32 * j + 16 * gl:32 * j + 16 * gl + 16, q, 64 * gl:64 * gl + 64] = bre[g].T
                bR_im[l, 32 * j + 16 * gl:32 * j + 16 * gl + 16, q, 64 * gl:64 * gl + 64] = bim[g].T
        for c in range(4):
            for j in range(4):
                for gl in range(2):
                    g = 8 * c + 2 * j + gl
                    aR_re[l, 32 * j:32 * j + 32, c, 64 * gl:64 * gl + 64] = are[g][None, :]
                    aR_im[l, 32 * j:32 * j + 32, c, 64 * gl:64 * gl + 64] = aim[g][None, :]
                    ldtR[l, 32 * j:32 * j + 32, c, 64 * gl:64 * gl + 64] = ldt[g]
    out.update(aR_re=aR_re, aR_im=aR_im, ldtR=ldtR, bR_re=bR_re, bR_im=bR_im, aS_re=aS_re, aS_im=aS_im, ldtS=ldtS,
               cS_re=cS_re, cS_im=cS_im)
    return out


_CACHE = {}


def run(inputs, debug=False, trace=False):
    x = np.asarray(inputs["x"], dtype=np.float32)
    B, LREAL, _ = x.shape
    key = (LREAL, debug)
    if key not in _CACHE:
        _CACHE[key] = build_program(LREAL, debug)
    nc = _CACHE[key]
    inp = {k: np.asarray(v) for k, v in inputs.items()}
    common = host_layout(inp, LREAL)
    in_maps = []
    for b in range(B):
        m = dict(common)
        m["xT"] = np.ascontiguousarray(x[b].T)
        in_maps.append(m)
    res = run_bass_kernel_spmd(nc, in_maps, core_ids=list(range(B)), trace=trace)
    return res


def kernel(**inputs):
    res = run(inputs)
    out = np.stack([np.ascontiguousarray(r["outT"].T) for r in res.results], 0)
    return out.astype(np.float32)
```

```python
import math, os
from contextlib import ExitStack
import numpy as np
import concourse.bass as bass
import concourse.mybir as mybir
from concourse.bass_utils import run_bass_kernel_spmd

F32 = mybir.dt.float32
BF16 = mybir.dt.bfloat16
ALU = mybir.AluOpType
AF = mybir.ActivationFunctionType

D = 1024; DC = 8; NM = 16; INW = 1216; QL = 384; KVL = 256; DR = 64; SW = 512
H = 4; DN = 128; DV = 128; FF = 2816; FC = 22; DEPTH = 2
EPS = 1e-6
SCALE = 1.0 / math.sqrt(192.0)
TS = 512


class Rec:
    ENG = ["pe", "act", "dve", "pool", "sp"]

    def __init__(self):
        self.ins = []
        self.lastw = {}
        self.readers = {}
        self.dmacnt = {}
        self.last_eng = {}
        self.last_dma = {}

    def _add(self, eng, fn, reads, writes, dma):
        d = set()
        for k in reads:
            if k in self.lastw:
                d.add(self.lastw[k])
        for k in writes:
            if k in self.lastw:
                d.add(self.lastw[k])
            d.update(self.readers.get(k, ()))
        i = len(self.ins)
        val = None
        if dma is not None:
            self.dmacnt[dma] = self.dmacnt.get(dma, 0) + 16
            val = self.dmacnt[dma]
        d |= set(self._extra)
        self.ins.append(dict(eng=eng, fn=fn, deps=d, dma=dma, val=val))
        if dma is not None:
            self.last_dma[dma] = i
        else:
            self.last_eng[eng] = i
        for k in reads:
            self.readers.setdefault(k, []).append(i)
        for k in writes:
            self.lastw[k] = i
            self.readers[k] = []
        return i

    def op(self, eng, fn, reads=(), writes=()):
        return self._add(eng, fn, reads, writes, None)

    def dma(self, eng, out, in_, sem, reads=(), writes=()):
        return self._add(eng, lambda e, o=out, i=in_: e.dma_start(out=o, in_=i), reads, writes, sem)

    _extra = ()

    def barrier(self):
        deps = list(self.last_eng.values()) + list(self.last_dma.values())
        self._extra = deps
        for e in self.ENG:
            self.op(e, lambda en: en.nop())
        self._extra = ()

    def emit(self, nc, stack):
        ins = self.ins
        has_cons = [False] * len(ins)
        for x in ins:
            for d in x["deps"]:
                has_cons[d] = True
        engsem = {e: stack.enter_context(nc.semaphore("es_" + e)) for e in self.ENG}
        dmasem = {k: stack.enter_context(nc.semaphore("ds_" + k)) for k in self.dmacnt}
        cnt = {e: 0 for e in self.ENG}
        for i, x in enumerate(ins):
            if x["dma"] is None and has_cons[i]:
                cnt[x["eng"]] += 1
                x["val"] = cnt[x["eng"]]
        per = {e: [i for i, x in enumerate(ins) if x["eng"] == e] for e in self.ENG}
        if os.environ.get("KDEBUG"):
            print("instr per engine", {e: len(v) for e, v in per.items()}, "sem counts", cnt, "dma", len(self.dmacnt), max(self.dmacnt.values()))
        outer = self

        def run(engname, eng):
            seen = {}
            for i in per[engname]:
                x = ins[i]
                waits = {}
                for d in x["deps"]:
                    p = ins[d]
                    if p["dma"] is not None:
                        key = ("d", p["dma"])
                    else:
                        if p["eng"] == "pe" and engname == "pe" and x["dma"] is None:
                            continue
                        key = ("e", p["eng"])
                    if p["val"] > waits.get(key, 0):
                        waits[key] = p["val"]
                for key, v in waits.items():
                    if seen.get(key, 0) >= v:
                        continue
                    seen[key] = v
                    sem = dmasem[key[1]] if key[0] == "d" else engsem[key[1]]
                    eng.wait_ge(sem, v)
                r = x["fn"](eng)
                if x["dma"] is not None:
                    r.then_inc(dmasem[x["dma"]], 16)
                elif has_cons[i]:
                    r.then_inc(engsem[engname], 1)
            return seen

        block = stack.enter_context(nc.Block())

        @block.tensor
        def _(e):
            run("pe", e)

        @block.scalar
        def _(e):
            run("act", e)

        @block.vector
        def _(e):
            run("dve", e)

        @block.gpsimd
        def _(e):
            run("pool", e)

        @block.sync
        def _(e):
            run("sp", e)
            for k, v in outer.dmacnt.items():
                e.wait_ge(dmasem[k], v)


class Arena:
    def __init__(self, nc, cap=229376):
        self.nc = nc; self.off = 16512; self.cap = cap; self.n = 0

    def alloc(self, shape, dtype, name=None):
        size = int(np.prod(shape[1:])) * (4 if dtype == F32 else 2)
        size = (size + 63) // 64 * 64
        assert self.off + size <= self.cap, ("SBUF overflow", name, self.off, size)
        self.n += 1
        t = self.nc.alloc_sbuf_tensor_at("%s_%d" % (name or "t", self.n), list(shape), dtype, offset=self.off)
        self.off += size
        return t

    def mark(self):
        return self.off

    def release(self, m):
        self.off = m


def tiles_of(L, ts):
    out = []
    t = 0
    while t < L:
        out.append((t, min(ts, L - t)))
        t += ts
    return out


def build_program(LREAL, debug=False):
    L = LREAL + NM
    nc = bass.Bass("TRN2", target_bir_lowering=False)
    R = Rec()
    A = Arena(nc)
    dr = {}

    def din(name, shape, dt=F32):
        dr[name] = nc.dram_tensor(name, list(shape), dt, kind="ExternalInput").ap()
        return dr[name]

    def dscr(name, shape, dt, out=False):
        dr[name] = nc.dram_tensor(name, list(shape), dt, kind=("ExternalOutput" if out else "Internal")).ap()
        return dr[name]

    xT = din("xT", [D, LREAL]); metaT = din("metaT", [D, NM])
    cosT = din("cosT", [DR, L]); sinT = din("sinT", [DR, L]); trim = din("trim", [128, 128])
    g_mix = din("g_mix", [DEPTH, 128, DC]); g_ffn = din("g_ffn", [DEPTH, 128, DC]); g_fin = din("g_fin", [128, DC])
    g_q = din("g_q", [DEPTH, 128, 3]); g_kv = din("g_kv", [DEPTH, 128, 2])
    g_ao = din("g_ao", [DEPTH, 128, 4]); g_so = din("g_so", [DEPTH, 128, 4]); d_skip = din("d_skip", [DEPTH, 128, 4])
    w_in = din("w_in", [DEPTH, D, INW]); w_uq = din("w_uq", [DEPTH, QL, 768]); w_ukv = din("w_ukv", [DEPTH, KVL, 1024])
    w_glu = din("w_glu", [DEPTH, SW, SW]); w_o = din("w_o", [DEPTH, D, D])
    w_gate = din("w_gate", [DEPTH, D, FF]); w_up = din("w_up", [DEPTH, D, FF]); w_down = din("w_down", [DEPTH, FF, D])
    aR_re = din("aR_re", [DEPTH, 128, 4, 128]); aR_im = din("aR_im", [DEPTH, 128, 4, 128]); ldtR = din("ldtR", [DEPTH, 128, 4, 128])
    bR_re = din("bR_re", [DEPTH, 128, 16, 128]); bR_im = din("bR_im", [DEPTH, 128, 16, 128])
    aS_re = din("aS_re", [DEPTH, 128, 16]); aS_im = din("aS_im", [DEPTH, 128, 16]); ldtS = din("ldtS", [DEPTH, 128, 16])
    cS_re = din("cS_re", [DEPTH, 128, 16, 128]); cS_im = din("cS_im", [DEPTH, 128, 16, 128])

    hT = dscr("hT", [D, L], F32, out=debug)
    qnT = dscr("qnT", [H, 128, L], BF16, out=debug); qrT = dscr("qrT", [H, 64, L], BF16, out=debug)
    kT = dscr("kT", [H, 128, L], BF16, out=debug); krT = dscr("krT", [64, L], BF16, out=debug)
    vv = dscr("vv", [L, 512], BF16, out=debug); uT = dscr("uT", [SW, L], BF16, out=debug)
    ssmT = dscr("ssmT", [SW, L], BF16, out=debug); attT = dscr("attT", [SW, L], BF16, out=debug)
    outT = dscr("outT", [D, LREAL], F32, out=True)

    stack = ExitStack()
    psb = [nc.alloc_psum_tensor("psb%d" % i, [128, 512], F32) for i in range(8)]
    pcount = [0]

    def nextps(lo=0, hi=8):
        i = lo + pcount[0] % (hi - lo)
        pcount[0] += 1
        return psb[i], "ps%d" % i

    def mm(out, lhsT, rhs, start, stop, reads, writes):
        R.op("pe", lambda e, o=out, l=lhsT, r=rhs, s=start, t=stop: e.matmul(o, l, r, start=s, stop=t), reads, writes)

    def act(out, in_, func, reads, writes, scale=1.0, bias=0.0):
        R.op("act", lambda e, o=out, i=in_, f=func, s=scale, b=bias: e.activation(o, i, f, bias=b, scale=s), reads, writes)

    def tt(eng, out, in0, in1, op, reads, writes):
        R.op(eng, lambda e, o=out, a=in0, b=in1, p=op: e.tensor_tensor(o, a, b, p), reads, writes)

    def ts(eng, out, in0, s1, s2, op0, op1, reads, writes):
        if s2 is None:
            R.op(eng, lambda e, o=out, a=in0, x=s1, p=op0: e.tensor_scalar(o, a, x, None, p), reads, writes)
        else:
            R.op(eng, lambda e, o=out, a=in0, x=s1, y=s2, p=op0, q=op1: e.tensor_scalar(o, a, x, y, p, q), reads, writes)

    def cp(eng, out, in_, reads, writes):
        if eng == "act":
            R.op("act", lambda e, o=out, i=in_: e.copy(o, i), reads, writes)
        else:
            R.op(eng, lambda e, o=out, i=in_: e.tensor_copy(o, i), reads, writes)

    def recip(out, in_, reads, writes):
        R.op("dve", lambda e, o=out, i=in_: e.reciprocal(o, i), reads, writes)

    def memset(eng, ap, val, writes):
        R.op(eng, lambda e, a=ap, v=val: e.memset(a, v), (), writes)

    ones_bf = A.alloc([128, 128], BF16, "ones"); memset("pool", ones_bf[:], 1.0, ["ones"])
    tri32 = A.alloc([128, 128], F32, "tri32"); tri_bf = A.alloc([128, 128], BF16, "tri")
    R.dma("sp", tri32[:], trim[:, :], "c_tri", (), ["tri32"])
    cp("pool", tri_bf[:], tri32[:], ["tri32"], ["tri"])
    gfin = A.alloc([128, DC], F32, "gfin"); R.dma("sp", gfin[:], g_fin[:, :], "c_gfin", (), ["gfin"])
    stcnt = [0]
    stage = [None, None]

    def alloc_stage():
        for i in range(2):
            stage[i] = A.alloc([128, FF], F32, "stage%d" % i)

    def load_cast(wap, rows, cols, dst, gain, tag, extra=None):
        for kc in range(rows // 128):
            s = stcnt[0] % 2; stcnt[0] += 1
            sk = "stage%d" % s
            R.dma("sp", stage[s][:, 0:cols], wap[kc * 128:(kc + 1) * 128, :], "ld_" + sk, (), [sk])
            eng = ["dve", "pool"][kc % 2]
            if gain is None:
                cp(eng, dst[:, kc, :], stage[s][:, 0:cols], [sk], [tag])
            else:
                ts(eng, dst[:, kc, :], stage[s][:, 0:cols], gain[:, kc:kc + 1], None, ALU.mult, None, [sk, tag + "_g"], [tag])
            if extra is not None:
                extra(kc, stage[s], sk)

    def rms_rstd(sq_list, nfeat, n, tagw, rstd_ap, rkey):
        ps, pk = nextps()
        for i, (sap, sk) in enumerate(sq_list):
            mm(ps[:, 0:n], ones_bf[:, :], sap, i == 0, i == len(sq_list) - 1, ["ones", sk], [pk])
        act(rstd_ap, ps[:, 0:n], AF.Sqrt, [pk], [rkey], scale=1.0 / nfeat, bias=EPS)
        recip(rstd_ap, rstd_ap, [rkey], [rkey])

    R.dma("sp", hT[:, 0:NM], metaT[:, :], "init_h", (), ["hT"])
    R.dma("sp", hT[:, NM:L], xT[:, :], "init_h", (), ["hT"])
    R.barrier()
    base_mark = A.mark()
    tl = tiles_of(L, TS)

    for l in range(DEPTH):
        A.release(base_mark)
        gmix = A.alloc([128, DC], F32, "gmix"); gq = A.alloc([128, 3], F32, "gq"); gkv = A.alloc([128, 2], F32, "gkv")
        R.dma("sp", gmix[:], g_mix[l], "c_g1", (), ["win_g"])
        R.dma("sp", gq[:], g_q[l], "c_g2", (), ["wuq_g"])
        R.dma("sp", gkv[:], g_kv[l], "c_g3", (), ["wukv_g"])
        win_bf = A.alloc([128, DC, INW], BF16, "win"); win_rot = A.alloc([128, DC, 64], BF16, "winrot")
        wuq_bf = A.alloc([128, 3, 768], BF16, "wuq"); wuq_rot = A.alloc([128, 3, 256], BF16, "wuqrot")
        wukv_bf = A.alloc([128, 2, 1024], BF16, "wukv"); wukv_v = A.alloc([128, 2, 512], BF16, "wukvv")
        alloc_stage()

        def ex_in(kc, st, sk):
            ts("dve", win_rot[:, kc, 0:32], st[:, 672:704], gmix[:, kc:kc + 1], -1.0, ALU.mult, ALU.mult, [sk, "win_g"], ["winrot"])
            ts("pool", win_rot[:, kc, 32:64], st[:, 640:672], gmix[:, kc:kc + 1], None, ALU.mult, None, [sk, "win_g"], ["winrot"])
        load_cast(w_in[l], D, INW, win_bf, gmix, "win", ex_in)

        def ex_uq(kc, st, sk):
            for h in range(H):
                b = 192 * h + 128
                ts("dve", wuq_rot[:, kc, 64 * h:64 * h + 32], st[:, b + 32:b + 64], gq[:, kc:kc + 1], -1.0, ALU.mult, ALU.mult, [sk, "wuq_g"], ["wuqrot"])
                ts("pool", wuq_rot[:, kc, 64 * h + 32:64 * h + 64], st[:, b:b + 32], gq[:, kc:kc + 1], None, ALU.mult, None, [sk, "wuq_g"], ["wuqrot"])
        load_cast(w_uq[l], QL, 768, wuq_bf, gq, "wuq", ex_uq)

        def ex_kv(kc, st, sk):
            for h in range(H):
                ts("pool", wukv_v[:, kc, 128 * h:128 * h + 128], st[:, 256 * h + 128:256 * h + 256], gkv[:, kc:kc + 1], None, ALU.mult, None, [sk, "wukv_g"], ["wukvv"])
        load_cast(w_ukv[l], KVL, 1024, wukv_bf, gkv, "wukv", ex_kv)

        h32 = A.alloc([128, DC, TS], F32, "h32"); xb = A.alloc([128, DC, TS], BF16, "xb"); sq = A.alloc([128, DC, TS], BF16, "sq")
        rstd = A.alloc([128, TS], F32, "rstd"); rstdq = A.alloc([128, TS], F32, "rstdq"); rstdkv = A.alloc([128, TS], F32, "rstdkv")
        cos2 = A.alloc([64, TS], F32, "cos2"); sin2 = A.alloc([64, TS], F32, "sin2")
        cq_bf = A.alloc([128, 3, TS], BF16, "cq"); sqq = A.alloc([128, 3, TS], BF16, "sqq")
        ckv_bf = A.alloc([128, 2, TS], BF16, "ckv"); sqkv = A.alloc([128, 2, TS], BF16, "sqkv")
        u_bf = A.alloc([128, 4, TS], BF16, "ubf"); kr_bf = A.alloc([64, TS], BF16, "krbf")
        rp1 = A.alloc([64, TS], F32, "rp1"); rp2 = A.alloc([64, TS], F32, "rp2")
        qn_bf = A.alloc([128, H, TS], BF16, "qnbf"); qr_bf = A.alloc([64, H, TS], BF16, "qrbf")
        kn_bf = A.alloc([128, H, TS], BF16, "knbf"); v_bf = A.alloc([128, 4, 512], BF16, "vbf")
        rtok = A.alloc([128, 4], F32, "rtok")
        ones_col = ones_bf[:, 0:1]

        for (t0, n) in tl:
            R.dma("sp", h32[:, :, 0:n], hT[:, t0:t0 + n].rearrange("(c p) t -> p c t", p=128), "ld_h32", (), ["h32"])
            R.dma("sp", cos2[:, 0:n], cosT[:, t0:t0 + n], "ld_cs", (), ["cos2"])
            R.dma("sp", sin2[:, 0:n], sinT[:, t0:t0 + n], "ld_cs", (), ["sin2"])
            for c in range(DC):
                cp("pool", xb[:, c, 0:n], h32[:, c, 0:n], ["h32"], ["xb%d" % c])
                act(sq[:, c, 0:n], h32[:, c, 0:n], AF.Square, ["h32"], ["sq%d" % c])
            rms_rstd([(sq[:, c, 0:n], "sq%d" % c) for c in range(DC)], D, n, "A", rstd[:, 0:n], "rstd")

            def proj(col0, m, wbf=win_bf, wtag="win"):
                ps, pk = nextps()
                for c in range(DC):
                    mm(ps[0:m, 0:n], wbf[:, c, col0:col0 + m], xb[:, c, 0:n], c == 0, c == DC - 1, [wtag, "xb%d" % c], [pk])
                return ps, pk
            for m in range(3):
                ps, pk = proj(128 * m, 128)
                tt("dve", cq_bf[:, m, 0:n], ps[:, 0:n], rstd[:, 0:n], ALU.mult, [pk, "rstd"], ["cq%d" % m])
                act(sqq[:, m, 0:n], cq_bf[:, m, 0:n], AF.Square, ["cq%d" % m], ["sqq%d" % m])
            for m in range(2):
                ps, pk = proj(QL + 128 * m, 128)
                tt("dve", ckv_bf[:, m, 0:n], ps[:, 0:n], rstd[:, 0:n], ALU.mult, [pk, "rstd"], ["ckv%d" % m])
                act(sqkv[:, m, 0:n], ckv_bf[:, m, 0:n], AF.Square, ["ckv%d" % m], ["sqkv%d" % m])
            ps, pk = proj(QL + KVL, 64)
            ps2, pk2 = proj(0, 64, win_rot, "winrot")
            tt("dve", rp1[:, 0:n], ps[0:64, 0:n], cos2[:, 0:n], ALU.mult, [pk, "cos2"], ["rp1"])
            tt("dve", rp2[:, 0:n], ps2[0:64, 0:n], sin2[:, 0:n], ALU.mult, [pk2, "sin2"], ["rp2"])
            tt("pool", rp1[:, 0:n], rp1[:, 0:n], rp2[:, 0:n], ALU.add, ["rp1", "rp2"], ["rp1"])
            tt("pool", kr_bf[:, 0:n], rp1[:, 0:n], rstd[0:64, 0:n], ALU.mult, ["rp1", "rstd"], ["krbf"])
            R.dma("sp", krT[:, t0:t0 + n], kr_bf[:, 0:n], "st_kr", ["krbf"], ["krT"])
            for m in range(4):
                ps, pk = proj(QL + KVL + DR + 128 * m, 128)
                tt("dve", u_bf[:, m, 0:n], ps[:, 0:n], rstd[:, 0:n], ALU.mult, [pk, "rstd"], ["ubf"])
            R.dma("sp", uT[:, t0:t0 + n].rearrange("(c p) t -> p c t", p=128), u_bf[:, :, 0:n], "st_u", ["ubf"], ["uT"])
            rms_rstd([(sqq[:, m, 0:n], "sqq%d" % m) for m in range(3)], QL, n, "Aq", rstdq[:, 0:n], "rstdq")
            for h in range(H):
                ps, pk = nextps()
                for c in range(3):
                    mm(ps[:, 0:n], wuq_bf[:, c, 192 * h:192 * h + 128], cq_bf[:, c, 0:n], c == 0, c == 2, ["wuq", "cq%d" % c], [pk])
                tt("dve", qn_bf[:, h, 0:n], ps[:, 0:n], rstdq[:, 0:n], ALU.mult, [pk, "rstdq"], ["qnbf"])
                ps, pk = nextps()
                for c in range(3):
                    mm(ps[0:64, 0:n], wuq_bf[:, c, 192 * h + 128:192 * h + 192], cq_bf[:, c, 0:n], c == 0, c == 2, ["wuq", "cq%d" % c], [pk])
                ps2, pk2 = nextps()
                for c in range(3):
                    mm(ps2[0:64, 0:n], wuq_rot[:, c, 64 * h:64 * h + 64], cq_bf[:, c, 0:n], c == 0, c == 2, ["wuqrot", "cq%d" % c], [pk2])
                tt("dve", rp1[:, 0:n], ps[0:64, 0:n], cos2[:, 0:n], ALU.mult, [pk, "cos2"], ["rp1"])
                tt("dve", rp2[:, 0:n], ps2[0:64, 0:n], sin2[:, 0:n], ALU.mult, [pk2, "sin2"], ["rp2"])
                tt("pool", rp1[:, 0:n], rp1[:, 0:n], rp2[:, 0:n], ALU.add, ["rp1", "rp2"], ["rp1"])
                tt("pool", qr_bf[:, h, 0:n], rp1[:, 0:n], rstdq[0:64, 0:n], ALU.mult, ["rp1", "rstdq"], ["qrbf"])
            R.dma("sp", qnT[:, :, t0:t0 + n].rearrange("h p t -> p h t"), qn_bf[:, :, 0:n], "st_qn", ["qnbf"], ["qnT"])
            R.dma("sp", qrT[:, :, t0:t0 + n].rearrange("h p t -> p h t"), qr_bf[:, :, 0:n], "st_qr", ["qrbf"], ["qrT"])
            rms_rstd([(sqkv[:, m, 0:n], "sqkv%d" % m) for m in range(2)], KVL, n, "Akv", rstdkv[:, 0:n], "rstdkv")
            for h in range(H):
                ps, pk = nextps()
                for c in range(2):
                    mm(ps[:, 0:n], wukv_bf[:, c, 256 * h:256 * h + 128], ckv_bf[:, c, 0:n], c == 0, c == 1, ["wukv", "ckv%d" % c], [pk])
                tt("dve", kn_bf[:, h, 0:n], ps[:, 0:n], rstdkv[:, 0:n], ALU.mult, [pk, "rstdkv"], ["knbf"])
            R.dma("sp", kT[:, :, t0:t0 + n].rearrange("h p t -> p h t"), kn_bf[:, :, 0:n], "st_kn", ["knbf"], ["kT"])
            nsb = (n + 127) // 128
            for s in range(nsb):
                m = min(128, n - 128 * s)
                ps, pk = nextps()
                for c in range(2):
                    mm(ps[0:m, 0:1], sqkv[:, c, 128 * s:128 * s + m], ones_col, c == 0, c == 1, ["ones", "sqkv%d" % c], [pk])
                act(rtok[0:m, s:s + 1], ps[0:m, 0:1], AF.Sqrt, [pk], ["rtok%d" % s], scale=1.0 / KVL, bias=EPS)
                recip(rtok[0:m, s:s + 1], rtok[0:m, s:s + 1], ["rtok%d" % s], ["rtok%d" % s])
                ps, pk = nextps()
                for c in range(2):
                    mm(ps[0:m, 0:512], ckv_bf[:, c, 128 * s:128 * s + m], wukv_v[:, c, :], c == 0, c == 1, ["wukvv", "ckv%d" % c], [pk])
                ts("dve", v_bf[0:m, s, :], ps[0:m, 0:512], rtok[0:m, s:s + 1], None, ALU.mult, None, [pk, "rtok%d" % s], ["vbf%d" % s])
                R.dma("sp", vv[t0 + 128 * s:t0 + 128 * s + m, :], v_bf[0:m, s, :], "st_v%d" % s, ["vbf%d" % s], ["vv"])
        R.barrier()
        if debug and debug == "A%d" % l:
            break

        A.release(base_mark)
        def lam_prep(shape, a_re_d, a_im_d, ldt_d, pfx):
            t = {}
            for nm in ["lr", "li", "dt", "mag", "c", "s", "t1", "t2", "lre", "lim"]:
                t[nm] = A.alloc(shape, F32, pfx + nm)
            k = lambda nm: pfx + nm
            R.dma("sp", t["lr"][:], a_re_d, "c_" + pfx + "1", (), [k("lr")])
            R.dma("sp", t["li"][:], a_im_d, "c_" + pfx + "2", (), [k("li")])
            R.dma("sp", t["dt"][:], ldt_d, "c_" + pfx + "3", (), [k("dt")])
            ts("dve", t["lr"][:], t["lr"][:], -1e-4, None, ALU.min, None, [k("lr")], [k("lr")])
            act(t["dt"][:], t["dt"][:], AF.Exp, [k("dt")], [k("dt")])
            tt("dve", t["t1"][:], t["lr"][:], t["dt"][:], ALU.mult, [k("lr"), k("dt")], [k("t1")])
            act(t["mag"][:], t["t1"][:], AF.Exp, [k("t1")], [k("mag")])
            tt("dve", t["t2"][:], t["li"][:], t["dt"][:], ALU.mult, [k("li"), k("dt")], [k("t2")])
            act(t["s"][:], t["t2"][:], AF.Sin, [k("t2")], [k("s")], scale=1.0 / 32.0)
            act(t["c"][:], t["t2"][:], AF.Sin, [k("t2")], [k("c")], scale=1.0 / 32.0, bias=math.pi / 2.0)
            for _ in range(5):
                tt("dve", t["t1"][:], t["c"][:], t["c"][:], ALU.mult, [k("c")], [k("t1")])
                tt("dve", t["t2"][:], t["s"][:], t["s"][:], ALU.mult, [k("s")], [k("t2")])
                tt("dve", t["s"][:], t["c"][:], t["s"][:], ALU.mult, [k("c"), k("s")], [k("s")])
                ts("dve", t["s"][:], t["s"][:], 2.0, None, ALU.mult, None, [k("s")], [k("s")])
                tt("dve", t["c"][:], t["t1"][:], t["t2"][:], ALU.subtract, [k("t1"), k("t2")], [k("c")])
            tt("dve", t["lre"][:], t["mag"][:], t["c"][:], ALU.mult, [k("mag"), k("c")], [k("lre")])
            tt("dve", t["lim"][:], t["mag"][:], t["s"][:], ALU.mult, [k("mag"), k("s")], [k("lim")])
            return t

        pm = A.mark()
        LBre = A.alloc([128, 16, 128], BF16, "LBre"); LBim = A.alloc([128, 16, 128], BF16, "LBim")
        LCre = A.alloc([128, 16, 128], BF16, "LCre"); LCim = A.alloc([128, 16, 128], BF16, "LCim")
        diagD = A.alloc([128, 4, 128], BF16, "diagD")
        Ere = A.alloc([128, 16, TS], BF16, "Ere"); Eim = A.alloc([128, 16, TS], BF16, "Eim")
        Elre = A.alloc([128, 16], F32, "Elre"); Elim = A.alloc([128, 16], F32, "Elim")
        rdec = A.alloc([128, 16], F32, "rdec")
        car_re = A.alloc([128, 16], F32, "carre"); car_im = A.alloc([128, 16], F32, "carim")
        wglu_bf = A.alloc([128, 4, SW], BF16, "wglu")
        dsk = A.alloc([128, 4], F32, "dsk")
        idt = A.alloc([128, 128], F32, "idt")
        pm2 = A.mark()
        tR = lam_prep([128, 4, 128], aR_re[l], aR_im[l], ldtR[l], "R")
        kR = lambda nm: "R" + nm
        den = A.alloc([128, 4, 128], F32, "den"); cre = A.alloc([128, 4, 128], F32, "cre"); cim = A.alloc([128, 4, 128], F32, "cim")
        b32r = A.alloc([128, 16, 128], F32, "b32r"); b32i = A.alloc([128, 16, 128], F32, "b32i")
        w1 = A.alloc([128, 16, 128], F32, "w1"); w2 = A.alloc([128, 16, 128], F32, "w2")
        R.dma("sp", b32r[:], bR_re[l], "c_b1", (), ["b32r"])
        R.dma("sp", b32i[:], bR_im[l], "c_b2", (), ["b32i"])
        tt("dve", den[:], tR["lr"][:], tR["lr"][:], ALU.mult, [kR("lr")], ["den"])
        tt("dve", tR["t1"][:], tR["li"][:], tR["li"][:], ALU.mult, [kR("li")], [kR("t1")])
        tt("dve", den[:], den[:], tR["t1"][:], ALU.add, ["den", kR("t1")], ["den"])
        recip(den[:], den[:], ["den"], ["den"])
        ts("dve", tR["lre"][:], tR["lre"][:], -1.0, None, ALU.add, None, [kR("lre")], [kR("lre")])
        tt("dve", tR["t1"][:], tR["lre"][:], tR["lr"][:], ALU.mult, [kR("lre"), kR("lr")], [kR("t1")])
        tt("dve", tR["t2"][:], tR["lim"][:], tR["li"][:], ALU.mult, [kR("lim"), kR("li")], [kR("t2")])
        tt("dve", tR["t1"][:], tR["t1"][:], tR["t2"][:], ALU.add, [kR("t1"), kR("t2")], [kR("t1")])
        tt("dve", cre[:], tR["t1"][:], den[:], ALU.mult, [kR("t1"), "den"], ["cre"])
        tt("dve", tR["t1"][:], tR["lim"][:], tR["lr"][:], ALU.mult, [kR("lim"), kR("lr")], [kR("t1")])
        tt("dve", tR["t2"][:], tR["lre"][:], tR["li"][:], ALU.mult, [kR("lre"), kR("li")], [kR("t2")])
        tt("dve", tR["t1"][:], tR["t1"][:], tR["t2"][:], ALU.subtract, [kR("t1"), kR("t2")], [kR("t1")])
        tt("dve", cim[:], tR["t1"][:], den[:], ALU.mult, [kR("t1"), "den"], ["cim"])
        for c in range(4):
            creb = cre[:, c:c + 1, :].to_broadcast([128, 4, 128]); cimb = cim[:, c:c + 1, :].to_broadcast([128, 4, 128])
            sl = slice(4 * c, 4 * c + 4)
            tt("dve", w1[:, sl, :], b32r[:, sl, :], creb, ALU.mult, ["b32r", "cre"], ["w1"])
            tt("dve", w2[:, sl, :], b32i[:, sl, :], cimb, ALU.mult, ["b32i", "cim"], ["w2"])
            tt("dve", LBre[:, sl, :], w1[:, sl, :], w2[:, sl, :], ALU.subtract, ["w1", "w2"], ["LBre"])
            tt("dve", w1[:, sl, :], b32i[:, sl, :], creb, ALU.mult, ["b32i", "cre"], ["w1"])
            tt("dve", w2[:, sl, :], b32r[:, sl, :], cimb, ALU.mult, ["b32r", "cim"], ["w2"])
            tt("dve", LBim[:, sl, :], w1[:, sl, :], w2[:, sl, :], ALU.add, ["w1", "w2"], ["LBim"])
        R.dma("sp", b32r[:], cS_re[l], "c_b1", (), ["b32r"])
        R.dma("sp", b32i[:], cS_im[l], "c_b2", (), ["b32i"])
        cp("pool", LCre[:], b32r[:], ["b32r"], ["LCre"])
        ts("pool", LCim[:], b32i[:], -1.0, None, ALU.mult, None, ["b32i"], ["LCim"])
        R.dma("sp", dsk[:], d_skip[l], "c_dsk", (), ["dsk"])
        R.barrier()
        A.release(pm2)
        tS = lam_prep([128, 16], aS_re[l], aS_im[l], ldtS[l], "S")
        kS = lambda nm: "S" + nm
        cp("dve", rdec[:], tS["mag"][:], [kS("mag")], ["rdec"])
        E32r = A.alloc([128, 16, TS], F32, "E32r"); E32i = A.alloc([128, 16, TS], F32, "E32i")
        tmpa = A.alloc([128, 16, TS // 2], F32, "tmpa"); tmpb = A.alloc([128, 16, TS // 2], F32, "tmpb")
        cp("dve", E32r[:, :, 0:1], tS["c"][:].unsqueeze(2), [kS("c")], ["E32r"])
        cp("dve", E32i[:, :, 0:1], tS["s"][:].unsqueeze(2), [kS("s")], ["E32i"])
        nn = 1
        while nn < TS:
            crb = E32r[:, :, nn - 1:nn].to_broadcast([128, 16, nn]); cib = E32i[:, :, nn - 1:nn].to_broadcast([128, 16, nn])
            tt("dve", tmpa[:, :, 0:nn], E32r[:, :, 0:nn], crb, ALU.mult, ["E32r"], ["tmpa"])
            tt("dve", tmpb[:, :, 0:nn], E32i[:, :, 0:nn], cib, ALU.mult, ["E32i"], ["tmpb"])
            tt("dve", tmpa[:, :, 0:nn], tmpa[:, :, 0:nn], tmpb[:, :, 0:nn], ALU.subtract, ["tmpa", "tmpb"], ["tmpa"])
            tt("dve", tmpb[:, :, 0:nn], E32r[:, :, 0:nn], cib, ALU.mult, ["E32r", "E32i"], ["tmpb"])
            cp("pool", E32r[:, :, nn:2 * nn], tmpa[:, :, 0:nn], ["tmpa"], ["E32r"])
            tt("dve", tmpa[:, :, 0:nn], E32i[:, :, 0:nn], crb, ALU.mult, ["E32i", "E32r"], ["tmpa"])
            tt("dve", E32i[:, :, nn:2 * nn], tmpa[:, :, 0:nn], tmpb[:, :, 0:nn], ALU.add, ["tmpa", "tmpb"], ["E32i"])
            nn *= 2
        cp("pool", Ere[:], E32r[:], ["E32r"], ["Ere"])
        cp("pool", Eim[:], E32i[:], ["E32i"], ["Eim"])
        cp("dve", Elre[:], E32r[:, :, TS - 1], ["E32r"], ["Elre"])
        cp("dve", Elim[:], E32i[:, :, TS - 1], ["E32i"], ["Elim"])
        memset("pool", car_re[:], 0.0, ["carre%d" % q for q in range(16)]); memset("pool", car_im[:], 0.0, ["carim%d" % q for q in range(16)])
        R.barrier()
        A.release(pm2)
        alloc_stage()
        load_cast(w_glu[l], SW, SW, wglu_bf, None, "wglu")
        memset("pool", idt[:], 0.0, ["idt"])
        cp("pool", idt[:, :], tri32[:, :], ["tri32", "idt"], ["idt"])
        tt("pool", idt[:, 1:128], tri32[:, 1:128], tri32[:, 0:127], ALU.subtract, ["tri32", "idt"], ["idt"])
        for c in range(4):
            ts("dve", diagD[:, c, :], idt[:, :], dsk[:, c:c + 1], None, ALU.mult, None, ["idt", "dsk"], ["diagD"])
        R.barrier()
        A.release(pm2)
        ub = A.alloc([128, 4, TS], BF16, "ub")
        m1 = A.alloc([128, TS], BF16, "m1"); m2 = A.alloc([128, TS], BF16, "m2")
        wre = [A.alloc([128, TS], BF16, "wre%d" % i) for i in range(2)]
        wim = [A.alloc([128, TS], BF16, "wim%d" % i) for i in range(2)]
        rb = A.alloc([128, TS], F32, "rb")
        z32r = [A.alloc([128, TS], F32, "z32r%d" % i) for i in range(2)]
        z32i = [A.alloc([128, TS], F32, "z32i%d" % i) for i in range(2)]
        zbr = [A.alloc([128, TS], BF16, "zbr%d" % i) for i in range(2)]
        zbi = [A.alloc([128, TS], BF16, "zbi%d" % i) for i in range(2)]
        xre = [A.alloc([128, TS], BF16, "xre%d" % i) for i in range(2)]
        xim = [A.alloc([128, TS], BF16, "xim%d" % i) for i in range(2)]
        cc = A.alloc([128, 8], F32, "cc")
        y32 = A.alloc([128, TS], F32, "y32"); y2 = A.alloc([128, TS], F32, "y2"); sg = A.alloc([128, TS], F32, "sg")
        g32 = A.alloc([128, 4, TS], F32, "g32"); g_bf = A.alloc([128, 4, TS], BF16, "gbf")
        s32 = A.alloc([128, 4, TS], F32, "s32"); sqs = A.alloc([128, 4, TS], BF16, "sqs")
        rstds = A.alloc([128, TS], F32, "rstds"); sn_bf = A.alloc([128, 4, TS], BF16, "snbf")

        for (t0, n) in tl:
            R.dma("sp", ub[:, :, 0:n], uT[:, t0:t0 + n].rearrange("(c p) t -> p c t", p=128), "ld_ub", (), ["ub"])
            for c in range(4):
                psy, pky = psb[6 + (c % 2)], "ps%d" % (6 + (c % 2))
                mm(psy[:, 0:n], diagD[:, c, :], ub[:, c, 0:n], True, False, ["diagD", "ub"], [pky])
                for j in range(4):
                    q = 4 * c + j
                    s = q % 2
                    sfx = "%d" % s
                    pr, pkr = nextps(0, 6); pi, pki = nextps(0, 6)
                    mm(pr[:, 0:n], LBre[:, q, :], ub[:, c, 0:n], True, True, ["LBre", "ub"], [pkr])
                    mm(pi[:, 0:n], LBim[:, q, :], ub[:, c, 0:n], True, True, ["LBim", "ub"], [pki])
                    tt("dve", m1[:, 0:n], pr[:, 0:n], Ere[:, q, 0:n], ALU.mult, [pkr, "Ere"], ["m1"])
                    tt("dve", m2[:, 0:n], pi[:, 0:n], Eim[:, q, 0:n], ALU.mult, [pki, "Eim"], ["m2"])
                    tt("dve", wre[s][:, 0:n], m1[:, 0:n], m2[:, 0:n], ALU.add, ["m1", "m2"], ["wre" + sfx])
                    tt("dve", m1[:, 0:n], pi[:, 0:n], Ere[:, q, 0:n], ALU.mult, [pki, "Ere"], ["m1"])
                    tt("dve", m2[:, 0:n], pr[:, 0:n], Eim[:, q, 0:n], ALU.mult, [pkr, "Eim"], ["m2"])
                    tt("dve", wim[s][:, 0:n], m1[:, 0:n], m2[:, 0:n], ALU.subtract, ["m1", "m2"], ["wim" + sfx])
                    cp("pool", rb[:, 0:n], rdec[:, q:q + 1].to_broadcast([128, n]), ["rdec"], ["rb"])
                    R.op("dve", lambda e, o=z32r[s][:, 0:n], a=rb[:, 0:n], b=wre[s][:, 0:n], i=car_re[:, q:q + 1]:
                         e.tensor_tensor_scan(o, a, b, i, ALU.mult, ALU.add), ["rb", "wre" + sfx, "carre%d" % q], ["z32r" + sfx])
                    R.op("dve", lambda e, o=z32i[s][:, 0:n], a=rb[:, 0:n], b=wim[s][:, 0:n], i=car_im[:, q:q + 1]:
                         e.tensor_tensor_scan(o, a, b, i, ALU.mult, ALU.add), ["rb", "wim" + sfx, "carim%d" % q], ["z32i" + sfx])
                    cp("pool", zbr[s][:, 0:n], z32r[s][:, 0:n], ["z32r" + sfx], ["zbr" + sfx])
                    cp("pool", zbi[s][:, 0:n], z32i[s][:, 0:n], ["z32i" + sfx], ["zbi" + sfx])
                    if n == TS:
                        zr = z32r[s][:, n - 1:n]; zi = z32i[s][:, n - 1:n]
                        ts("pool", cc[:, 0:1], zr, Elre[:, q:q + 1], None, ALU.mult, None, ["z32r" + sfx, "Elre"], ["cc0"])
                        ts("pool", cc[:, 1:2], zi, Elim[:, q:q + 1], None, ALU.mult, None, ["z32i" + sfx, "Elim"], ["cc1"])
                        ts("pool", cc[:, 2:3], zr, Elim[:, q:q + 1], None, ALU.mult, None, ["z32r" + sfx, "Elim"], ["cc2"])
                        ts("pool", cc[:, 3:4], zi, Elre[:, q:q + 1], None, ALU.mult, None, ["z32i" + sfx, "Elre"], ["cc3"])
                        tt("pool", car_re[:, q:q + 1], cc[:, 0:1], cc[:, 1:2], ALU.subtract, ["cc0", "cc1"], ["carre%d" % q])
                        tt("pool", car_im[:, q:q + 1], cc[:, 2:3], cc[:, 3:4], ALU.add, ["cc2", "cc3"], ["carim%d" % q])
                    tt("dve", m1[:, 0:n], zbr[s][:, 0:n], Ere[:, q, 0:n], ALU.mult, ["zbr" + sfx, "Ere"], ["m1"])
                    tt("dve", m2[:, 0:n], zbi[s][:, 0:n], Eim[:, q, 0:n], ALU.mult, ["zbi" + sfx, "Eim"], ["m2"])
                    tt("dve", xre[s][:, 0:n], m1[:, 0:n], m2[:, 0:n], ALU.subtract, ["m1", "m2"], ["xre" + sfx])
                    tt("dve", m1[:, 0:n], zbr[s][:, 0:n], Eim[:, q, 0:n], ALU.mult, ["zbr" + sfx, "Eim"], ["m1"])
                    tt("dve", m2[:, 0:n], zbi[s][:, 0:n], Ere[:, q, 0:n], ALU.mult, ["zbi" + sfx, "Ere"], ["m2"])
                    tt("dve", xim[s][:, 0:n], m1[:, 0:n], m2[:, 0:n], ALU.add, ["m1", "m2"], ["xim" + sfx])
                    mm(psy[:, 0:n], LCre[:, q, :], xre[s][:, 0:n], False, False, ["LCre", "xre" + sfx], [pky])
                    mm(psy[:, 0:n], LCim[:, q, :], xim[s][:, 0:n], False, j == 3, ["LCim", "xim" + sfx], [pky])
                act(y32[:, 0:n], psy[:, 0:n], AF.Identity, [pky], ["y32"])
                act(y2[:, 0:n], psy[:, 0:n], AF.Square, [pky], ["y2"])
                ts("pool", y2[:, 0:n], y2[:, 0:n], 0.044715, 1.0, ALU.mult, ALU.add, ["y2"], ["y2"])
                tt("pool", y2[:, 0:n], y2[:, 0:n], y32[:, 0:n], ALU.mult, ["y2", "y32"], ["y2"])
                act(sg[:, 0:n], y2[:, 0:n], AF.Sigmoid, ["y2"], ["sg"], scale=2.0 * math.sqrt(2.0 / math.pi))
                tt("pool", g32[:, c, 0:n], y32[:, 0:n], sg[:, 0:n], ALU.mult, ["y32", "sg"], ["g32_%d" % c])
                cp("pool", g_bf[:, c, 0:n], g32[:, c, 0:n], ["g32_%d" % c], ["gbf%d" % c])
            for oc in range(4):
                ps, pk = nextps(0, 6)
                for kc in range(4):
                    mm(ps[:, 0:n], wglu_bf[:, kc, 128 * oc:128 * oc + 128], g_bf[:, kc, 0:n], kc == 0, kc == 3, ["wglu", "gbf%d" % kc], [pk])
                act(sg[:, 0:n], ps[:, 0:n], AF.Sigmoid, [pk], ["sg"])
                tt("pool", s32[:, oc, 0:n], g32[:, oc, 0:n], sg[:, 0:n], ALU.mult, ["g32_%d" % oc, "sg"], ["s32_%d" % oc])
                act(sqs[:, oc, 0:n], s32[:, oc, 0:n], AF.Square, ["s32_%d" % oc], ["sqs%d" % oc])
            rms_rstd([(sqs[:, oc, 0:n], "sqs%d" % oc) for oc in range(4)], SW, n, "Bs", rstds[:, 0:n], "rstds")
            for oc in range(4):
                tt("pool", sn_bf[:, oc, 0:n], s32[:, oc, 0:n], rstds[:, 0:n], ALU.mult, ["s32_%d" % oc, "rstds"], ["snbf"])
            R.dma("sp", ssmT[:, t0:t0 + n].rearrange("(c p) t -> p c t", p=128), sn_bf[:, :, 0:n], "st_sn", ["snbf"], ["ssmT"])
        R.barrier()
        if debug and debug == "B%d" % l:
            break

        A.release(base_mark)
        NKB = (L + 127) // 128
        kres = A.alloc([128, H, L], BF16, "kres"); krres = A.alloc([64, L], BF16, "krres")
        vres = A.alloc([128, NKB, 512], BF16, "vres")
        for h in range(H):
            R.dma("sp", kres[:, h, :], kT[h], "ld_kres", (), ["kres"])
        R.dma("sp", krres[:, :], krT[:, :], "ld_kres", (), ["kres"])
        nfull = L // 128
        R.dma("sp", vres[:, 0:nfull, :], vv[0:nfull * 128, :].rearrange("(b p) d -> p b d", p=128), "ld_kres", (), ["kres"])
        if L % 128:
            R.dma("sp", vres[0:L % 128, nfull, :], vv[nfull * 128:L, :], "ld_kres", (), ["kres"])
        qn = [A.alloc([128, H, TS], BF16, "qn%d" % i) for i in range(2)]
        qr = [A.alloc([64, H, TS], BF16, "qr%d" % i) for i in range(2)]
        NPT = 4; LA = 2
        pT = [A.alloc([128, TS], BF16, "pT%d" % i) for i in range(NPT)]
        rinv = [A.alloc([128, TS], F32, "rinv%d" % i) for i in range(2)]
        at32 = [A.alloc([128, H, TS], F32, "at32_%d" % i) for i in range(2)]
        sqa = [A.alloc([128, H, TS], BF16, "sqa%d" % i) for i in range(2)]
        rstda = A.alloc([128, TS], F32, "rstda"); an_bf = A.alloc([128, H, TS], BF16, "anbf")
        units = []
        for ti, (t0, n) in enumerate(tl):
            kbmax = (t0 + n - 1) // 128
            for h in range(H):
                for kb in range(kbmax + 1):
                    units.append((ti, t0, n, h, kb, kbmax))
        scnt = [0]
        slot_of = {}

        def load_q(ti):
            if ti >= len(tl):
                return
            t0, n = tl[ti]; p_ = ti % 2
            R.dma("sp", qn[p_][:, :, 0:n], qnT[:, :, t0:t0 + n].rearrange("h p t -> p h t"), "ld_qn%d" % p_, (), ["qn%d" % p_])
            R.dma("sp", qr[p_][:, :, 0:n], qrT[:, :, t0:t0 + n].rearrange("h p t -> p h t"), "ld_qr%d" % p_, (), ["qr%d" % p_])

        def emit_S(ui):
            ti, t0, n, h, kb, kbmax = units[ui]
            p_ = ti % 2
            if h == 0 and kb == 0:
                if ti == 0:
                    load_q(0)
                load_q(ti + 1)
            ks = min(128, L - 128 * kb); c0 = max(0, 128 * kb - t0)
            sb = scnt[0] % NPT; scnt[0] += 1
            slot_of[ui] = sb
            pss, pks = psb[sb], "ps%d" % sb
            mm(pss[0:ks, c0:n], kres[:, h, 128 * kb:128 * kb + ks], qn[p_][:, h, c0:n], True, False, ["kres", "qn%d" % p_], [pks])
            mm(pss[0:ks, c0:n], krres[:, 128 * kb:128 * kb + ks], qr[p_][:, h, c0:n], False, True, ["kres", "qr%d" % p_], [pks])
            pk_ = "pT%d" % sb
            act(pT[sb][0:ks, c0:n], pss[0:ks, c0:n], AF.Exp, [pks], [pk_], scale=SCALE)
            if 128 * kb >= t0:
                w = min(128, n - c0)
                tt("pool", pT[sb][0:ks, c0:c0 + w], pT[sb][0:ks, c0:c0 + w], tri_bf[0:ks, 0:w], ALU.mult, [pk_, "tri"], [pk_])

        def emit_PV(ui):
            ti, t0, n, h, kb, kbmax = units[ui]
            p_ = ti % 2
            ks = min(128, L - 128 * kb); c0 = max(0, 128 * kb - t0)
            sb = slot_of[ui]; pk_ = "pT%d" % sb
            po, pko = psb[4 + (h % 2)], "ps%d" % (4 + (h % 2))
            prr, pkrr = psb[6 + (h % 2)], "ps%d" % (6 + (h % 2))
            last = kb == kbmax
            mm(po[:, c0:n], vres[0:ks, kb, 128 * h:128 * h + 128], pT[sb][0:ks, c0:n], kb == 0, last, ["kres", pk_], [pko])
            mm(prr[:, c0:n], ones_bf[0:ks, :], pT[sb][0:ks, c0:n], kb == 0, last, ["ones", pk_], [pkrr])
            if last:
                hp = h % 2
                recip(rinv[hp][:, 0:n], prr[:, 0:n], [pkrr], ["rinv%d" % hp])
                tt("dve", at32[p_][:, h, 0:n], po[:, 0:n], rinv[hp][:, 0:n], ALU.mult, [pko, "rinv%d" % hp], ["at32_%d_%d" % (p_, h)])
                act(sqa[p_][:, h, 0:n], at32[p_][:, h, 0:n], AF.Square, ["at32_%d_%d" % (p_, h)], ["sqa%d_%d" % (p_, h)])
                if h == H - 1:
                    sb2 = scnt[0] % NPT; scnt[0] += 1
                    ps, pk = psb[sb2], "ps%d" % sb2
                    for hh_ in range(H):
                        mm(ps[:, 0:n], ones_bf[:, :], sqa[p_][:, hh_, 0:n], hh_ == 0, hh_ == H - 1, ["ones", "sqa%d_%d" % (p_, hh_)], [pk])
                    act(rstda[:, 0:n], ps[:, 0:n], AF.Sqrt, [pk], ["rstda"], scale=1.0 / SW, bias=EPS)
                    recip(rstda[:, 0:n], rstda[:, 0:n], ["rstda"], ["rstda"])
                    for hh_ in range(H):
                        tt("pool", an_bf[:, hh_, 0:n], at32[p_][:, hh_, 0:n], rstda[:, 0:n], ALU.mult, ["at32_%d_%d" % (p_, hh_), "rstda"], ["anbf"])
                    R.dma("sp", attT[:, t0:t0 + n].rearrange("(c p) t -> p c t", p=128), an_bf[:, :, 0:n], "st_an", ["anbf"], ["attT"])

        for idx in range(len(units) + LA):
            if idx < len(units):
                emit_S(idx)
            if idx - LA >= 0:
                emit_PV(idx - LA)
        R.barrier()
        if debug and debug == "C%d" % l:
            break

        A.release(base_mark)
        T2 = 256
        gao = A.alloc([128, 8], F32, "gao")
        R.dma("sp", gao[:, 0:4], g_ao[l], "c_gao", (), ["wo_g"])
        R.dma("sp", gao[:, 4:8], g_so[l], "c_gao2", (), ["wo_g"])
        gff = A.alloc([128, DC], F32, "gff"); R.dma("sp", gff[:], g_ffn[l], "c_gff", (), ["wg_g", "wu_g"])
        wo_bf = A.alloc([128, 8, D], BF16, "wo"); wg_bf = A.alloc([128, DC, FF], BF16, "wg")
        wu_bf = A.alloc([128, DC, FF], BF16, "wu"); wd_bf = A.alloc([128, FC, D], BF16, "wd")
        pm3 = A.mark()
        alloc_stage()
        load_cast(w_o[l], D, D, wo_bf, gao, "wo")
        load_cast(w_gate[l], D, FF, wg_bf, gff, "wg")
        load_cast(w_up[l], D, FF, wu_bf, gff, "wu")
        load_cast(w_down[l], FF, D, wd_bf, None, "wd")
        R.barrier()
        A.release(pm3)
        hhb = [A.alloc([128, DC, T2], F32, "hh%d" % i) for i in range(2)]
        mixb = [A.alloc([128, 8, T2], BF16, "mix%d" % i) for i in range(2)]
        xb2 = A.alloc([128, DC, T2], BF16, "xb2"); sq2 = A.alloc([128, DC, T2], BF16, "sq2")
        rstdf = A.alloc([128, T2], F32, "rstdf")
        gs = A.alloc([128, T2], F32, "gs"); us = A.alloc([128, T2], F32, "us"); sl_ = A.alloc([128, T2], F32, "sl")
        a_bf = A.alloc([128, FC, T2], BF16, "abf")
        tl2 = tiles_of(L, T2)

        def load_c2(ti):
            if ti >= len(tl2):
                return
            t0, n = tl2[ti]; p_ = ti % 2
            R.dma("sp", hhb[p_][:, :, 0:n], hT[:, t0:t0 + n].rearrange("(c p) t -> p c t", p=128), "ld_hh%d" % p_, (),
                  ["hh_%d" % p_] + ["hh_%d_%d" % (p_, oc) for oc in range(DC)])
            R.dma("sp", mixb[p_][:, 0:4, 0:n], attT[:, t0:t0 + n].rearrange("(c p) t -> p c t", p=128), "ld_mix%d" % p_, (), ["mixa%d" % p_])
            R.dma("sp", mixb[p_][:, 4:8, 0:n], ssmT[:, t0:t0 + n].rearrange("(c p) t -> p c t", p=128), "ld_mixs%d" % p_, (), ["mixs%d" % p_])

        load_c2(0)
        for ti, (t0, n) in enumerate(tl2):
            p_ = ti % 2
            hh = hhb[p_]; mix = mixb[p_]
            hk = lambda oc: "hh_%d_%d" % (p_, oc)
            load_c2(ti + 1)
            for oc in range(DC):
                ps, pk = nextps()
                for kc in range(8):
                    mm(ps[:, 0:n], wo_bf[:, kc, 128 * oc:128 * oc + 128], mix[:, kc, 0:n], kc == 0, kc == 7, ["wo", "mixa%d" % p_, "mixs%d" % p_], [pk])
                tt("dve", hh[:, oc, 0:n], ps[:, 0:n], hh[:, oc, 0:n], ALU.add, [pk, hk(oc)], [hk(oc)])
                cp("pool", xb2[:, oc, 0:n], hh[:, oc, 0:n], [hk(oc)], ["xb2_%d" % oc])
                act(sq2[:, oc, 0:n], hh[:, oc, 0:n], AF.Square, [hk(oc)], ["sq2_%d" % oc])
            rms_rstd([(sq2[:, c, 0:n], "sq2_%d" % c) for c in range(DC)], D, n, "F", rstdf[:, 0:n], "rstdf")
            for fc in range(FC):
                pg, pkg = nextps(); pu, pku = nextps()
                for c in range(DC):
                    mm(pg[:, 0:n], wg_bf[:, c, 128 * fc:128 * fc + 128], xb2[:, c, 0:n], c == 0, c == DC - 1, ["wg", "xb2_%d" % c], [pkg])
                for c in range(DC):
                    mm(pu[:, 0:n], wu_bf[:, c, 128 * fc:128 * fc + 128], xb2[:, c, 0:n], c == 0, c == DC - 1, ["wu", "xb2_%d" % c], [pku])
                tt("dve", gs[:, 0:n], pg[:, 0:n], rstdf[:, 0:n], ALU.mult, [pkg, "rstdf"], ["gs"])
                tt("dve", us[:, 0:n], pu[:, 0:n], rstdf[:, 0:n], ALU.mult, [pku, "rstdf"], ["us"])
                act(sl_[:, 0:n], gs[:, 0:n], AF.Silu, ["gs"], ["sl"])
                tt("pool", a_bf[:, fc, 0:n], sl_[:, 0:n], us[:, 0:n], ALU.mult, ["sl", "us"], ["abf%d" % fc])
            last_layer = (l == DEPTH - 1)
            for oc in range(DC):
                ps, pk = nextps()
                for fc in range(FC):
                    mm(ps[:, 0:n], wd_bf[:, fc, 128 * oc:128 * oc + 128], a_bf[:, fc, 0:n], fc == 0, fc == FC - 1, ["wd", "abf%d" % fc], [pk])
                tt("dve", hh[:, oc, 0:n], ps[:, 0:n], hh[:, oc, 0:n], ALU.add, [pk, hk(oc)], [hk(oc)])
                if last_layer:
                    act(sq2[:, oc, 0:n], hh[:, oc, 0:n], AF.Square, [hk(oc)], ["sq2_%d" % oc])
            allh = ["hh_%d" % p_] + [hk(oc) for oc in range(DC)]
            if not last_layer:
                R.dma("sp", hT[:, t0:t0 + n].rearrange("(c p) t -> p c t", p=128), hh[:, :, 0:n], "st_hh%d" % p_, allh, ["hT"])
            else:
                rms_rstd([(sq2[:, c, 0:n], "sq2_%d" % c) for c in range(DC)], D, n, "G", rstdf[:, 0:n], "rstdf")
                for oc in range(DC):
                    R.op("dve", lambda e, o=hh[:, oc, 0:n], a=hh[:, oc, 0:n], s=gfin[:, oc:oc + 1], b=rstdf[:, 0:n]:
                         e.scalar_tensor_tensor(o, a, s, b, ALU.mult, ALU.mult), [hk(oc), "rstdf", "gfin"], [hk(oc)])
                a0 = max(t0, NM)
                if t0 + n > a0:
                    R.dma("sp", outT[:, a0 - NM:t0 + n - NM].rearrange("(c p) t -> p c t", p=128), hh[:, :, a0 - t0:n], "st_o%d" % p_,
                          allh, ["outT"])
        R.barrier()

    R.emit(nc, stack)
    stack.close()
    return nc


def _lay_vec(v, C):
    return np.ascontiguousarray(v.reshape(C, 128).T)


def host_layout(inp, LREAL):
    L = LREAL + NM
    f = np.float32
    out = {}
    out["metaT"] = np.ascontiguousarray(inp["meta_tokens"].T.astype(f))
    pos = np.arange(L, dtype=f)
    inv = (1.0 / (np.float32(10000.0) ** (np.arange(0, DR, 2, dtype=f) / np.float32(DR)))).astype(f)
    ang = (pos[:, None] * inv[None, :]).astype(f)
    c = np.cos(ang).astype(f).T; s = np.sin(ang).astype(f).T
    out["cosT"] = np.ascontiguousarray(np.concatenate([c, c], 0)); out["sinT"] = np.ascontiguousarray(np.concatenate([s, s], 0))
    k = np.arange(128)
    out["trim"] = (k[None, :] >= k[:, None]).astype(f)
    out["g_mix"] = np.stack([_lay_vec(inp["norm_mix_g"][l], DC) for l in range(DEPTH)])
    out["g_ffn"] = np.stack([_lay_vec(inp["norm_ffn_g"][l], DC) for l in range(DEPTH)])
    out["g_fin"] = _lay_vec(inp["final_norm_g"], DC)
    out["g_q"] = np.stack([_lay_vec(inp["q_norm_g"][l], 3) for l in range(DEPTH)])
    out["g_kv"] = np.stack([_lay_vec(inp["kv_norm_g"][l], 2) for l in range(DEPTH)])
    out["g_ao"] = np.stack([_lay_vec(inp["attn_out_g"][l], 4) for l in range(DEPTH)])
    out["g_so"] = np.stack([_lay_vec(inp["ssm_out_g"][l], 4) for l in range(DEPTH)])
    out["d_skip"] = np.stack([_lay_vec(inp["ssm_d"][l], 4) for l in range(DEPTH)])
    for nm in ["w_in", "w_uq", "w_ukv", "w_glu", "w_o", "w_gate", "w_up", "w_down"]:
        out[nm] = np.ascontiguousarray(inp[nm].astype(f))
    aR_re = np.zeros((DEPTH, 128, 4, 128), f); aR_im = np.zeros_like(aR_re); ldtR = np.zeros_like(aR_re)
    bR_re = np.zeros((DEPTH, 128, 16, 128), f); bR_im = np.zeros_like(bR_re)
    aS_re = np.zeros((DEPTH, 128, 16), f); aS_im = np.zeros_like(aS_re); ldtS = np.zeros_like(aS_re)
    cS_re = np.zeros((DEPTH, 128, 16, 128), f); cS_im = np.zeros_like(cS_re)
    for l in range(DEPTH):
        are, aim, ldt = inp["ssm_a_re"][l], inp["ssm_a_im"][l], inp["ssm_log_dt"][l]
        bre, bim, cre, cim = inp["ssm_b_re"][l], inp["ssm_b_im"][l], inp["ssm_c_re"][l], inp["ssm_c_im"][l]
        for q in range(16):
            for gl in range(2):
                g = 2 * q + gl
                aS_re[l, 64 * gl:64 * gl + 64, q] = are[g]; aS_im[l, 64 * gl:64 * gl + 64, q] = aim[g]
                ldtS[l, 64 * gl:64 * gl + 64, q] = ldt[g]
                j = q % 4
                cS_re[l, 64 * gl:64 * gl + 64, q, 32 * j + 16 * gl:32 * j + 16 * gl + 16] = cre[g].T
                cS_im[l, 64 * gl:64 * gl + 64, q, 32 * j + 16 * gl:32 * j + 16 * gl + 16] = cim[g].T
                bR_re[l, 32 * j + 16 * gl:32 * j + 16 * gl + 16, q, 64 * gl:64 * gl + 64] = bre[g].T
                bR_im[l, 32 * j + 16 * gl:32 * j + 16 * gl + 16, q, 64 * gl:64 * gl + 64] = bim[g].T
        for c in range(4):
            for j in range(4):
                for gl in range(2):
                    g = 8 * c + 2 * j + gl
                    aR_re[l, 32 * j:32 * j + 32, c, 64 * gl:64 * gl + 64] = are[g][None, :]
                    aR_im[l, 32 * j:32 * j + 32, c, 64 * gl:64 * gl + 64] = aim[g][None, :]
                    ldtR[l, 32 * j:32 * j + 32, c, 64 * gl:64 * gl + 64] = ldt[g]
    out.update(aR_re=aR_re, aR_im=aR_im, ldtR=ldtR, bR_re=bR_re, bR_im=bR_im, aS_re=aS_re, aS_im=aS_im, ldtS=ldtS,
               cS_re=cS_re, cS_im=cS_im)
    return out


_CACHE = {}


def run(inputs, debug=False, trace=False):
    x = np.asarray(inputs["x"], dtype=np.float32)
    B, LREAL, _ = x.shape
    key = (LREAL, debug)
    if key not in _CACHE:
        _CACHE[key] = build_program(LREAL, debug)
    nc = _CACHE[key]
    inp = {k: np.asarray(v) for k, v in inputs.items()}
    common = host_layout(inp, LREAL)
    in_maps = []
    for b in range(B):
        m = dict(common)
        m["xT"] = np.ascontiguousarray(x[b].T)
        in_maps.append(m)
    res = run_bass_kernel_spmd(nc, in_maps, core_ids=list(range(B)), trace=trace)
    return res


def kernel(**inputs):
    res = run(inputs)
    out = np.stack([np.ascontiguousarray(r["outT"].T) for r in res.results], 0)
    return out.astype(np.float32)
```

```python
import math, os
from contextlib import ExitStack
import numpy as np
import concourse.bass as bass
import concourse.mybir as mybir
from concourse.bass_utils import run_bass_kernel_spmd

F32 = mybir.dt.float32
BF16 = mybir.dt.bfloat16
ALU = mybir.AluOpType
AF = mybir.ActivationFunctionType

D = 1024; DC = 8; NM = 16; INW = 1216; QL = 384; KVL = 256; DR = 64; SW = 512
H = 4; DN = 128; DV = 128; FF = 2816; FC = 22; DEPTH = 2
EPS = 1e-6
SCALE = 1.0 / math.sqrt(192.0)
TS = 512


class Rec:
    ENG = ["pe", "act", "dve", "pool", "sp"]

    def __init__(self):
        self.ins = []
        self.lastw = {}
        self.readers = {}
        self.dmacnt = {}
        self.last_eng = {}
        self.last_dma = {}

    def _add(self, eng, fn, reads, writes, dma):
        d = set()
        for k in reads:
            if k in self.lastw:
                d.add(self.lastw[k])
        for k in writes:
            if k in self.lastw:
                d.add(self.lastw[k])
            d.update(self.readers.get(k, ()))
        i = len(self.ins)
        val = None
        if dma is not None:
            self.dmacnt[dma] = self.dmacnt.get(dma, 0) + 16
            val = self.dmacnt[dma]
        d |= set(self._extra)
        self.ins.append(dict(eng=eng, fn=fn, deps=d, dma=dma, val=val))
        if dma is not None:
            self.last_dma[dma] = i
        else:
            self.last_eng[eng] = i
        for k in reads:
            self.readers.setdefault(k, []).append(i)
        for k in writes:
            self.lastw[k] = i
            self.readers[k] = []
        return i

    def op(self, eng, fn, reads=(), writes=()):
        return self._add(eng, fn, reads, writes, None)

    def dma(self, eng, out, in_, sem, reads=(), writes=()):
        return self._add(eng, lambda e, o=out, i=in_: e.dma_start(out=o, in_=i), reads, writes, sem)

    _extra = ()

    def barrier(self):
        deps = list(self.last_eng.values()) + list(self.last_dma.values())
        self._extra = deps
        for e in self.ENG:
            self.op(e, lambda en: en.nop())
        self._extra = ()

    def emit(self, nc, stack):
        ins = self.ins
        has_cons = [False] * len(ins)
        for x in ins:
            for d in x["deps"]:
                has_cons[d] = True
        engsem = {e: stack.enter_context(nc.semaphore("es_" + e)) for e in self.ENG}
        dmasem = {k: stack.enter_context(nc.semaphore("ds_" + k)) for k in self.dmacnt}
        cnt = {e: 0 for e in self.ENG}
        for i, x in enumerate(ins):
            if x["dma"] is None and has_cons[i]:
                cnt[x["eng"]] += 1
                x["val"] = cnt[x["eng"]]
        per = {e: [i for i, x in enumerate(ins) if x["eng"] == e] for e in self.ENG}
        if os.environ.get("KDEBUG"):
            print("instr per engine", {e: len(v) for e, v in per.items()}, "sem counts", cnt, "dma", len(self.dmacnt), max(self.dmacnt.values()))
        outer = self

        def run(engname, eng):
            seen = {}
            for i in per[engname]:
                x = ins[i]
                waits = {}
                for d in x["deps"]:
                    p = ins[d]
                    if p["dma"] is not None:
                        key = ("d", p["dma"])
                    else:
                        if p["eng"] == "pe" and engname == "pe" and x["dma"] is None:
                            continue
                        key = ("e", p["eng"])
                    if p["val"] > waits.get(key, 0):
                        waits[key] = p["val"]
                for key, v in waits.items():
                    if seen.get(key, 0) >= v:
                        continue
                    seen[key] = v
                    sem = dmasem[key[1]] if key[0] == "d" else engsem[key[1]]
                    eng.wait_ge(sem, v)
                r = x["fn"](eng)
                if x["dma"] is not None:
                    r.then_inc(dmasem[x["dma"]], 16)
                elif has_cons[i]:
                    r.then_inc(engsem[engname], 1)
            return seen

        block = stack.enter_context(nc.Block())

        @block.tensor
        def _(e):
            run("pe", e)

        @block.scalar
        def _(e):
            run("act", e)

        @block.vector
        def _(e):
            run("dve", e)

        @block.gpsimd
        def _(e):
            run("pool", e)

        @block.sync
        def _(e):
            run("sp", e)
            for k, v in outer.dmacnt.items():
                e.wait_ge(dmasem[k], v)


class Arena:
    def __init__(self, nc, cap=229376):
        self.nc = nc; self.off = 16512; self.cap = cap; self.n = 0

    def alloc(self, shape, dtype, name=None):
        size = int(np.prod(shape[1:])) * (4 if dtype == F32 else 2)
        size = (size + 63) // 64 * 64
        assert self.off + size <= self.cap, ("SBUF overflow", name, self.off, size)
        self.n += 1
        t = self.nc.alloc_sbuf_tensor_at("%s_%d" % (name or "t", self.n), list(shape), dtype, offset=self.off)
        self.off += size
        return t

    def mark(self):
        return self.off

    def release(self, m):
        self.off = m


def tiles_of(L, ts):
    out = []
    t = 0
    while t < L:
        out.append((t, min(ts, L - t)))
        t += ts
    return out


def build_program(LREAL, debug=False):
    L = LREAL + NM
    nc = bass.Bass("TRN2", target_bir_lowering=False)
    R = Rec()
    A = Arena(nc)
    dr = {}

    def din(name, shape, dt=F32):
        dr[name] = nc.dram_tensor(name, list(shape), dt, kind="ExternalInput").ap()
        return dr[name]

    def dscr(name, shape, dt, out=False):
        dr[name] = nc.dram_tensor(name, list(shape), dt, kind=("ExternalOutput" if out else "Internal")).ap()
        return dr[name]

    xT = din("xT", [D, LREAL]); metaT = din("metaT", [D, NM])
    cosT = din("cosT", [DR, L]); sinT = din("sinT", [DR, L]); trim = din("trim", [128, 128])
    g_mix = din("g_mix", [DEPTH, 128, DC]); g_ffn = din("g_ffn", [DEPTH, 128, DC]); g_fin = din("g_fin", [128, DC])
    g_q = din("g_q", [DEPTH, 128, 3]); g_kv = din("g_kv", [DEPTH, 128, 2])
    g_ao = din("g_ao", [DEPTH, 128, 4]); g_so = din("g_so", [DEPTH, 128, 4]); d_skip = din("d_skip", [DEPTH, 128, 4])
    w_in = din("w_in", [DEPTH, D, INW]); w_uq = din("w_uq", [DEPTH, QL, 768]); w_ukv = din("w_ukv", [DEPTH, KVL, 1024])
    w_glu = din("w_glu", [DEPTH, SW, SW]); w_o = din("w_o", [DEPTH, D, D])
    w_gate = din("w_gate", [DEPTH, D, FF]); w_up = din("w_up", [DEPTH, D, FF]); w_down = din("w_down", [DEPTH, FF, D])
    aR_re = din("aR_re", [DEPTH, 128, 4, 128]); aR_im = din("aR_im", [DEPTH, 128, 4, 128]); ldtR = din("ldtR", [DEPTH, 128, 4, 128])
    bR_re = din("bR_re", [DEPTH, 128, 16, 128]); bR_im = din("bR_im", [DEPTH, 128, 16, 128])
    aS_re = din("aS_re", [DEPTH, 128, 16]); aS_im = din("aS_im", [DEPTH, 128, 16]); ldtS = din("ldtS", [DEPTH, 128, 16])
    cS_re = din("cS_re", [DEPTH, 128, 16, 128]); cS_im = din("cS_im", [DEPTH, 128, 16, 128])

    hT = dscr("hT", [D, L], F32, out=debug)
    qnT = dscr("qnT", [H, 128, L], BF16, out=debug); qrT = dscr("qrT", [H, 64, L], BF16, out=debug)
    kT = dscr("kT", [H, 128, L], BF16, out=debug); krT = dscr("krT", [64, L], BF16, out=debug)
    vv = dscr("vv", [L, 512], BF16, out=debug); uT = dscr("uT", [SW, L], BF16, out=debug)
    ssmT = dscr("ssmT", [SW, L], BF16, out=debug); attT = dscr("attT", [SW, L], BF16, out=debug)
    outT = dscr("outT", [D, LREAL], F32, out=True)

    stack = ExitStack()
    psb = [nc.alloc_psum_tensor("psb%d" % i, [128, 512], F32) for i in range(8)]
    pcount = [0]

    def nextps(lo=0, hi=8):
        i = lo + pcount[0] % (hi - lo)
        pcount[0] += 1
        return psb[i], "ps%d" % i

    def mm(out, lhsT, rhs, start, stop, reads, writes):
        R.op("pe", lambda e, o=out, l=lhsT, r=rhs, s=start, t=stop: e.matmul(o, l, r, start=s, stop=t), reads, writes)

    def act(out, in_, func, reads, writes, scale=1.0, bias=0.0):
        R.op("act", lambda e, o=out, i=in_, f=func, s=scale, b=bias: e.activation(o, i, f, bias=b, scale=s), reads, writes)

    def tt(eng, out, in0, in1, op, reads, writes):
        R.op(eng, lambda e, o=out, a=in0, b=in1, p=op: e.tensor_tensor(o, a, b, p), reads, writes)

    def ts(eng, out, in0, s1, s2, op0, op1, reads, writes):
        if s2 is None:
            R.op(eng, lambda e, o=out, a=in0, x=s1, p=op0: e.tensor_scalar(o, a, x, None, p), reads, writes)
        else:
            R.op(eng, lambda e, o=out, a=in0, x=s1, y=s2, p=op0, q=op1: e.tensor_scalar(o, a, x, y, p, q), reads, writes)

    def cp(eng, out, in_, reads, writes):
        if eng == "act":
            R.op("act", lambda e, o=out, i=in_: e.activation(o, i, AF.Identity, bias=0.0, scale=1.0), reads, writes)
        else:
            R.op(eng, lambda e, o=out, i=in_: e.tensor_copy(o, i), reads, writes)

    def recip(out, in_, reads, writes):
        R.op("dve", lambda e, o=out, i=in_: e.reciprocal(o, i), reads, writes)

    def memset(eng, ap, val, writes):
        R.op(eng, lambda e, a=ap, v=val: e.memset(a, v), (), writes)

    ones_bf = A.alloc([128, 128], BF16, "ones"); memset("pool", ones_bf[:], 1.0, ["ones"])
    tri32 = A.alloc([128, 128], F32, "tri32"); tri_bf = A.alloc([128, 128], BF16, "tri")
    R.dma("sp", tri32[:], trim[:, :], "c_tri", (), ["tri32"])
    cp("pool", tri_bf[:], tri32[:], ["tri32"], ["tri"])
    gfin = A.alloc([128, DC], F32, "gfin"); R.dma("sp", gfin[:], g_fin[:, :], "c_gfin", (), ["gfin"])
    stcnt = [0]
    NST = 4
    stage = [None] * NST

    def alloc_stage():
        for i in range(NST):
            stage[i] = A.alloc([128, FF], F32, "stage%d" % i)

    def load_cast(wap, rows, cols, dst, gain, tag, extra=None):
        for kc in range(rows // 128):
            s = stcnt[0] % NST; stcnt[0] += 1
            sk = "stage%d" % s
            R.dma("sp", stage[s][:, 0:cols], wap[kc * 128:(kc + 1) * 128, :], "ld_" + sk, (), [sk])
            eng = ["dve", "pool", "act"][stcnt[0] % 3]
            if gain is None:
                cp(eng, dst[:, kc, :], stage[s][:, 0:cols], [sk], [tag])
            elif eng == "act":
                R.op("act", lambda e, o=dst[:, kc, :], i=stage[s][:, 0:cols], g=gain[:, kc:kc + 1]: e.activation(o, i, AF.Identity, bias=0.0, scale=g), [sk, tag + "_g"], [tag])
            else:
                ts(eng, dst[:, kc, :], stage[s][:, 0:cols], gain[:, kc:kc + 1], None, ALU.mult, None, [sk, tag + "_g"], [tag])
            if extra is not None:
                extra(kc, stage[s], sk)

    def rms_rstd(sq_list, nfeat, n, tagw, rstd_ap, rkey):
        ps, pk = nextps()
        for i, (sap, sk) in enumerate(sq_list):
            mm(ps[:, 0:n], ones_bf[:, :], sap, i == 0, i == len(sq_list) - 1, ["ones", sk], [pk])
        act(rstd_ap, ps[:, 0:n], AF.Sqrt, [pk], [rkey], scale=1.0 / nfeat, bias=EPS)
        recip(rstd_ap, rstd_ap, [rkey], [rkey])

    R.dma("sp", hT[:, 0:NM], metaT[:, :], "init_h", (), ["hT"])
    R.dma("sp", hT[:, NM:L], xT[:, :], "init_h", (), ["hT"])
    R.barrier()
    base_mark = A.mark()
    tl = tiles_of(L, TS)

    for l in range(DEPTH):
        A.release(base_mark)
        gmix = A.alloc([128, DC], F32, "gmix"); gq = A.alloc([128, 3], F32, "gq"); gkv = A.alloc([128, 2], F32, "gkv")
        R.dma("sp", gmix[:], g_mix[l], "c_g1", (), ["win_g"])
        R.dma("sp", gq[:], g_q[l], "c_g2", (), ["wuq_g"])
        R.dma("sp", gkv[:], g_kv[l], "c_g3", (), ["wukv_g"])
        win_bf = A.alloc([128, DC, INW], BF16, "win"); win_rot = A.alloc([128, DC, 64], BF16, "winrot")
        wuq_bf = A.alloc([128, 3, 768], BF16, "wuq"); wuq_rot = A.alloc([128, 3, 256], BF16, "wuqrot")
        wukv_bf = A.alloc([128, 2, 1024], BF16, "wukv"); wukv_v = A.alloc([128, 2, 512], BF16, "wukvv")
        alloc_stage()

        def ex_in(kc, st, sk):
            ts("dve", win_rot[:, kc, 0:32], st[:, 672:704], gmix[:, kc:kc + 1], -1.0, ALU.mult, ALU.mult, [sk, "win_g"], ["winrot"])
            ts("pool", win_rot[:, kc, 32:64], st[:, 640:672], gmix[:, kc:kc + 1], None, ALU.mult, None, [sk, "win_g"], ["winrot"])
        load_cast(w_in[l], D, INW, win_bf, gmix, "win", ex_in)

        def ex_uq(kc, st, sk):
            for h in range(H):
                b = 192 * h + 128
                ts("dve", wuq_rot[:, kc, 64 * h:64 * h + 32], st[:, b + 32:b + 64], gq[:, kc:kc + 1], -1.0, ALU.mult, ALU.mult, [sk, "wuq_g"], ["wuqrot"])
                ts("pool", wuq_rot[:, kc, 64 * h + 32:64 * h + 64], st[:, b:b + 32], gq[:, kc:kc + 1], None, ALU.mult, None, [sk, "wuq_g"], ["wuqrot"])
        load_cast(w_uq[l], QL, 768, wuq_bf, gq, "wuq", ex_uq)

        def ex_kv(kc, st, sk):
            for h in range(H):
                ts("pool", wukv_v[:, kc, 128 * h:128 * h + 128], st[:, 256 * h + 128:256 * h + 256], gkv[:, kc:kc + 1], None, ALU.mult, None, [sk, "wukv_g"], ["wukvv"])
        load_cast(w_ukv[l], KVL, 1024, wukv_bf, gkv, "wukv", ex_kv)

        h32 = A.alloc([128, DC, TS], F32, "h32"); xb = A.alloc([128, DC, TS], BF16, "xb"); sq = A.alloc([128, DC, TS], BF16, "sq")
        rstd = A.alloc([128, TS], F32, "rstd"); rstdq = A.alloc([128, TS], F32, "rstdq"); rstdkv = A.alloc([128, TS], F32, "rstdkv")
        cos2 = A.alloc([64, TS], F32, "cos2"); sin2 = A.alloc([64, TS], F32, "sin2")
        cq_bf = A.alloc([128, 3, TS], BF16, "cq"); sqq = A.alloc([128, 3, TS], BF16, "sqq")
        ckv_bf = A.alloc([128, 2, TS], BF16, "ckv"); sqkv = A.alloc([128, 2, TS], BF16, "sqkv")
        u_bf = A.alloc([128, 4, TS], BF16, "ubf"); kr_bf = A.alloc([64, TS], BF16, "krbf")
        rp1 = A.alloc([64, TS], F32, "rp1"); rp2 = A.alloc([64, TS], F32, "rp2")
        qn_bf = A.alloc([128, H, TS], BF16, "qnbf"); qr_bf = A.alloc([64, H, TS], BF16, "qrbf")
        kn_bf = A.alloc([128, H, TS], BF16, "knbf"); v_bf = A.alloc([128, 4, 512], BF16, "vbf")
        rtok = A.alloc([128, 4], F32, "rtok")
        ones_col = ones_bf[:, 0:1]

        for (t0, n) in tl:
            R.dma("sp", h32[:, :, 0:n], hT[:, t0:t0 + n].rearrange("(c p) t -> p c t", p=128), "ld_h32", (), ["h32"])
            R.dma("sp", cos2[:, 0:n], cosT[:, t0:t0 + n], "ld_cs", (), ["cos2"])
            R.dma("sp", sin2[:, 0:n], sinT[:, t0:t0 + n], "ld_cs", (), ["sin2"])
            for c in range(DC):
                cp("pool", xb[:, c, 0:n], h32[:, c, 0:n], ["h32"], ["xb%d" % c])
                act(sq[:, c, 0:n], h32[:, c, 0:n], AF.Square, ["h32"], ["sq%d" % c])
            rms_rstd([(sq[:, c, 0:n], "sq%d" % c) for c in range(DC)], D, n, "A", rstd[:, 0:n], "rstd")

            def proj(col0, m, wbf=win_bf, wtag="win"):
                ps, pk = nextps()
                for c in range(DC):
                    mm(ps[0:m, 0:n], wbf[:, c, col0:col0 + m], xb[:, c, 0:n], c == 0, c == DC - 1, [wtag, "xb%d" % c], [pk])
                return ps, pk
            for m in range(3):
                ps, pk = proj(128 * m, 128)
                tt("dve", cq_bf[:, m, 0:n], ps[:, 0:n], rstd[:, 0:n], ALU.mult, [pk, "rstd"], ["cq%d" % m])
                act(sqq[:, m, 0:n], cq_bf[:, m, 0:n], AF.Square, ["cq%d" % m], ["sqq%d" % m])
            for m in range(2):
                ps, pk = proj(QL + 128 * m, 128)
                tt("dve", ckv_bf[:, m, 0:n], ps[:, 0:n], rstd[:, 0:n], ALU.mult, [pk, "rstd"], ["ckv%d" % m])
                act(sqkv[:, m, 0:n], ckv_bf[:, m, 0:n], AF.Square, ["ckv%d" % m], ["sqkv%d" % m])
            ps, pk = proj(QL + KVL, 64)
            ps2, pk2 = proj(0, 64, win_rot, "winrot")
            tt("dve", rp1[:, 0:n], ps[0:64, 0:n], cos2[:, 0:n], ALU.mult, [pk, "cos2"], ["rp1"])
            tt("dve", rp2[:, 0:n], ps2[0:64, 0:n], sin2[:, 0:n], ALU.mult, [pk2, "sin2"], ["rp2"])
            tt("pool", rp1[:, 0:n], rp1[:, 0:n], rp2[:, 0:n], ALU.add, ["rp1", "rp2"], ["rp1"])
            tt("pool", kr_bf[:, 0:n], rp1[:, 0:n], rstd[0:64, 0:n], ALU.mult, ["rp1", "rstd"], ["krbf"])
            R.dma("sp", krT[:, t0:t0 + n], kr_bf[:, 0:n], "st_kr", ["krbf"], ["krT"])
            for m in range(4):
                ps, pk = proj(QL + KVL + DR + 128 * m, 128)
                tt("dve", u_bf[:, m, 0:n], ps[:, 0:n], rstd[:, 0:n], ALU.mult, [pk, "rstd"], ["ubf"])
            R.dma("sp", uT[:, t0:t0 + n].rearrange("(c p) t -> p c t", p=128), u_bf[:, :, 0:n], "st_u", ["ubf"], ["uT"])
            rms_rstd([(sqq[:, m, 0:n], "sqq%d" % m) for m in range(3)], QL, n, "Aq", rstdq[:, 0:n], "rstdq")
            for h in range(H):
                ps, pk = nextps()
                for c in range(3):
                    mm(ps[:, 0:n], wuq_bf[:, c, 192 * h:192 * h + 128], cq_bf[:, c, 0:n], c == 0, c == 2, ["wuq", "cq%d" % c], [pk])
                tt("dve", qn_bf[:, h, 0:n], ps[:, 0:n], rstdq[:, 0:n], ALU.mult, [pk, "rstdq"], ["qnbf"])
                ps, pk = nextps()
                for c in range(3):
                    mm(ps[0:64, 0:n], wuq_bf[:, c, 192 * h + 128:192 * h + 192], cq_bf[:, c, 0:n], c == 0, c == 2, ["wuq", "cq%d" % c], [pk])
                ps2, pk2 = nextps()
                for c in range(3):
                    mm(ps2[0:64, 0:n], wuq_rot[:, c, 64 * h:64 * h + 64], cq_bf[:, c, 0:n], c == 0, c == 2, ["wuqrot", "cq%d" % c], [pk2])
                tt("dve", rp1[:, 0:n], ps[0:64, 0:n], cos2[:, 0:n], ALU.mult, [pk, "cos2"], ["rp1"])
                tt("dve", rp2[:, 0:n], ps2[0:64, 0:n], sin2[:, 0:n], ALU.mult, [pk2, "sin2"], ["rp2"])
                tt("pool", rp1[:, 0:n], rp1[:, 0:n], rp2[:, 0:n], ALU.add, ["rp1", "rp2"], ["rp1"])
                tt("pool", qr_bf[:, h, 0:n], rp1[:, 0:n], rstdq[0:64, 0:n], ALU.mult, ["rp1", "rstdq"], ["qrbf"])
            R.dma("sp", qnT[:, :, t0:t0 + n].rearrange("h p t -> p h t"), qn_bf[:, :, 0:n], "st_qn", ["qnbf"], ["qnT"])
            R.dma("sp", qrT[:, :, t0:t0 + n].rearrange("h p t -> p h t"), qr_bf[:, :, 0:n], "st_qr", ["qrbf"], ["qrT"])
            rms_rstd([(sqkv[:, m, 0:n], "sqkv%d" % m) for m in range(2)], KVL, n, "Akv", rstdkv[:, 0:n], "rstdkv")
            for h in range(H):
                ps, pk = nextps()
                for c in range(2):
                    mm(ps[:, 0:n], wukv_bf[:, c, 256 * h:256 * h + 128], ckv_bf[:, c, 0:n], c == 0, c == 1, ["wukv", "ckv%d" % c], [pk])
                tt("dve", kn_bf[:, h, 0:n], ps[:, 0:n], rstdkv[:, 0:n], ALU.mult, [pk, "rstdkv"], ["knbf"])
            R.dma("sp", kT[:, :, t0:t0 + n].rearrange("h p t -> p h t"), kn_bf[:, :, 0:n], "st_kn", ["knbf"], ["kT"])
            nsb = (n + 127) // 128
            for s in range(nsb):
                m = min(128, n - 128 * s)
                ps, pk = nextps()
                for c in range(2):
                    mm(ps[0:m, 0:1], sqkv[:, c, 128 * s:128 * s + m], ones_col, c == 0, c == 1, ["ones", "sqkv%d" % c], [pk])
                act(rtok[0:m, s:s + 1], ps[0:m, 0:1], AF.Sqrt, [pk], ["rtok%d" % s], scale=1.0 / KVL, bias=EPS)
                recip(rtok[0:m, s:s + 1], rtok[0:m, s:s + 1], ["rtok%d" % s], ["rtok%d" % s])
                ps, pk = nextps()
                for c in range(2):
                    mm(ps[0:m, 0:512], ckv_bf[:, c, 128 * s:128 * s + m], wukv_v[:, c, :], c == 0, c == 1, ["wukvv", "ckv%d" % c], [pk])
                ts("dve", v_bf[0:m, s, :], ps[0:m, 0:512], rtok[0:m, s:s + 1], None, ALU.mult, None, [pk, "rtok%d" % s], ["vbf%d" % s])
                R.dma("sp", vv[t0 + 128 * s:t0 + 128 * s + m, :], v_bf[0:m, s, :], "st_v%d" % s, ["vbf%d" % s], ["vv"])
        R.barrier()
        if debug and debug == "A%d" % l:
            break

        A.release(base_mark)
        def lam_prep(shape, a_re_d, a_im_d, ldt_d, pfx):
            t = {}
            for nm in ["lr", "li", "dt", "mag", "c", "s", "t1", "t2", "lre", "lim"]:
                t[nm] = A.alloc(shape, F32, pfx + nm)
            k = lambda nm: pfx + nm
            R.dma("sp", t["lr"][:], a_re_d, "c_" + pfx + "1", (), [k("lr")])
            R.dma("sp", t["li"][:], a_im_d, "c_" + pfx + "2", (), [k("li")])
            R.dma("sp", t["dt"][:], ldt_d, "c_" + pfx + "3", (), [k("dt")])
            ts("dve", t["lr"][:], t["lr"][:], -1e-4, None, ALU.min, None, [k("lr")], [k("lr")])
            act(t["dt"][:], t["dt"][:], AF.Exp, [k("dt")], [k("dt")])
            tt("dve", t["t1"][:], t["lr"][:], t["dt"][:], ALU.mult, [k("lr"), k("dt")], [k("t1")])
            act(t["mag"][:], t["t1"][:], AF.Exp, [k("t1")], [k("mag")])
            tt("dve", t["t2"][:], t["li"][:], t["dt"][:], ALU.mult, [k("li"), k("dt")], [k("t2")])
            act(t["s"][:], t["t2"][:], AF.Sin, [k("t2")], [k("s")], scale=1.0 / 32.0)
            act(t["c"][:], t["t2"][:], AF.Sin, [k("t2")], [k("c")], scale=1.0 / 32.0, bias=math.pi / 2.0)
            for _ in range(5):
                tt("dve", t["t1"][:], t["c"][:], t["c"][:], ALU.mult, [k("c")], [k("t1")])
                tt("dve", t["t2"][:], t["s"][:], t["s"][:], ALU.mult, [k("s")], [k("t2")])
                tt("dve", t["s"][:], t["c"][:], t["s"][:], ALU.mult, [k("c"), k("s")], [k("s")])
                ts("dve", t["s"][:], t["s"][:], 2.0, None, ALU.mult, None, [k("s")], [k("s")])
                tt("dve", t["c"][:], t["t1"][:], t["t2"][:], ALU.subtract, [k("t1"), k("t2")], [k("c")])
            tt("dve", t["lre"][:], t["mag"][:], t["c"][:], ALU.mult, [k("mag"), k("c")], [k("lre")])
            tt("dve", t["lim"][:], t["mag"][:], t["s"][:], ALU.mult, [k("mag"), k("s")], [k("lim")])
            return t

        pm = A.mark()
        LBre = A.alloc([128, 16, 128], BF16, "LBre"); LBim = A.alloc([128, 16, 128], BF16, "LBim")
        LCre = A.alloc([128, 16, 128], BF16, "LCre"); LCim = A.alloc([128, 16, 128], BF16, "LCim")
        diagD = A.alloc([128, 4, 128], BF16, "diagD")
        Ere = A.alloc([128, 16, TS], BF16, "Ere"); Eim = A.alloc([128, 16, TS], BF16, "Eim")
        Elre = A.alloc([128, 16], F32, "Elre"); Elim = A.alloc([128, 16], F32, "Elim")
        rdec = A.alloc([128, 16], F32, "rdec")
        car_re = A.alloc([128, 16], F32, "carre"); car_im = A.alloc([128, 16], F32, "carim")
        wglu_bf = A.alloc([128, 4, SW], BF16, "wglu")
        dsk = A.alloc([128, 4], F32, "dsk")
        idt = A.alloc([128, 128], F32, "idt")
        pm2 = A.mark()
        tR = lam_prep([128, 4, 128], aR_re[l], aR_im[l], ldtR[l], "R")
        kR = lambda nm: "R" + nm
        den = A.alloc([128, 4, 128], F32, "den"); cre = A.alloc([128, 4, 128], F32, "cre"); cim = A.alloc([128, 4, 128], F32, "cim")
        b32r = A.alloc([128, 16, 128], F32, "b32r"); b32i = A.alloc([128, 16, 128], F32, "b32i")
        w1 = A.alloc([128, 16, 128], F32, "w1"); w2 = A.alloc([128, 16, 128], F32, "w2")
        R.dma("sp", b32r[:], bR_re[l], "c_b1", (), ["b32r"])
        R.dma("sp", b32i[:], bR_im[l], "c_b2", (), ["b32i"])
        tt("dve", den[:], tR["lr"][:], tR["lr"][:], ALU.mult, [kR("lr")], ["den"])
        tt("dve", tR["t1"][:], tR["li"][:], tR["li"][:], ALU.mult, [kR("li")], [kR("t1")])
        tt("dve", den[:], den[:], tR["t1"][:], ALU.add, ["den", kR("t1")], ["den"])
        recip(den[:], den[:], ["den"], ["den"])
        ts("dve", tR["lre"][:], tR["lre"][:], -1.0, None, ALU.add, None, [kR("lre")], [kR("lre")])
        tt("dve", tR["t1"][:], tR["lre"][:], tR["lr"][:], ALU.mult, [kR("lre"), kR("lr")], [kR("t1")])
        tt("dve", tR["t2"][:], tR["lim"][:], tR["li"][:], ALU.mult, [kR("lim"), kR("li")], [kR("t2")])
        tt("dve", tR["t1"][:], tR["t1"][:], tR["t2"][:], ALU.add, [kR("t1"), kR("t2")], [kR("t1")])
        tt("dve", cre[:], tR["t1"][:], den[:], ALU.mult, [kR("t1"), "den"], ["cre"])
        tt("dve", tR["t1"][:], tR["lim"][:], tR["lr"][:], ALU.mult, [kR("lim"), kR("lr")], [kR("t1")])
        tt("dve", tR["t2"][:], tR["lre"][:], tR["li"][:], ALU.mult, [kR("lre"), kR("li")], [kR("t2")])
        tt("dve", tR["t1"][:], tR["t1"][:], tR["t2"][:], ALU.subtract, [kR("t1"), kR("t2")], [kR("t1")])
        tt("dve", cim[:], tR["t1"][:], den[:], ALU.mult, [kR("t1"), "den"], ["cim"])
        for c in range(4):
            creb = cre[:, c:c + 1, :].to_broadcast([128, 4, 128]); cimb = cim[:, c:c + 1, :].to_broadcast([128, 4, 128])
            sl = slice(4 * c, 4 * c + 4)
            tt("dve", w1[:, sl, :], b32r[:, sl, :], creb, ALU.mult, ["b32r", "cre"], ["w1"])
            tt("dve", w2[:, sl, :], b32i[:, sl, :], cimb, ALU.mult, ["b32i", "cim"], ["w2"])
            tt("dve", LBre[:, sl, :], w1[:, sl, :], w2[:, sl, :], ALU.subtract, ["w1", "w2"], ["LBre"])
            tt("dve", w1[:, sl, :], b32i[:, sl, :], creb, ALU.mult, ["b32i", "cre"], ["w1"])
            tt("dve", w2[:, sl, :], b32r[:, sl, :], cimb, ALU.mult, ["b32r", "cim"], ["w2"])
            tt("dve", LBim[:, sl, :], w1[:, sl, :], w2[:, sl, :], ALU.add, ["w1", "w2"], ["LBim"])
        R.dma("sp", b32r[:], cS_re[l], "c_b1", (), ["b32r"])
        R.dma("sp", b32i[:], cS_im[l], "c_b2", (), ["b32i"])
        cp("pool", LCre[:], b32r[:], ["b32r"], ["LCre"])
        ts("pool", LCim[:], b32i[:], -1.0, None, ALU.mult, None, ["b32i"], ["LCim"])
        R.dma("sp", dsk[:], d_skip[l], "c_dsk", (), ["dsk"])
        R.barrier()
        A.release(pm2)
        tS = lam_prep([128, 16], aS_re[l], aS_im[l], ldtS[l], "S")
        kS = lambda nm: "S" + nm
        cp("dve", rdec[:], tS["mag"][:], [kS("mag")], ["rdec"])
        E32r = A.alloc([128, 16, TS], F32, "E32r"); E32i = A.alloc([128, 16, TS], F32, "E32i")
        tmpa = A.alloc([128, 16, TS // 2], F32, "tmpa"); tmpb = A.alloc([128, 16, TS // 2], F32, "tmpb")
        cp("dve", E32r[:, :, 0:1], tS["c"][:].unsqueeze(2), [kS("c")], ["E32r"])
        cp("dve", E32i[:, :, 0:1], tS["s"][:].unsqueeze(2), [kS("s")], ["E32i"])
        nn = 1
        while nn < TS:
            crb = E32r[:, :, nn - 1:nn].to_broadcast([128, 16, nn]); cib = E32i[:, :, nn - 1:nn].to_broadcast([128, 16, nn])
            tt("dve", tmpa[:, :, 0:nn], E32r[:, :, 0:nn], crb, ALU.mult, ["E32r"], ["tmpa"])
            tt("dve", tmpb[:, :, 0:nn], E32i[:, :, 0:nn], cib, ALU.mult, ["E32i"], ["tmpb"])
            tt("dve", tmpa[:, :, 0:nn], tmpa[:, :, 0:nn], tmpb[:, :, 0:nn], ALU.subtract, ["tmpa", "tmpb"], ["tmpa"])
            tt("dve", tmpb[:, :, 0:nn], E32r[:, :, 0:nn], cib, ALU.mult, ["E32r", "E32i"], ["tmpb"])
            cp("pool", E32r[:, :, nn:2 * nn], tmpa[:, :, 0:nn], ["tmpa"], ["E32r"])
            tt("dve", tmpa[:, :, 0:nn], E32i[:, :, 0:nn], crb, ALU.mult, ["E32i", "E32r"], ["tmpa"])
            tt("dve", E32i[:, :, nn:2 * nn], tmpa[:, :, 0:nn], tmpb[:, :, 0:nn], ALU.add, ["tmpa", "tmpb"], ["E32i"])
            nn *= 2
        cp("pool", Ere[:], E32r[:], ["E32r"], ["Ere"])
        cp("pool", Eim[:], E32i[:], ["E32i"], ["Eim"])
        cp("dve", Elre[:], E32r[:, :, TS - 1], ["E32r"], ["Elre"])
        cp("dve", Elim[:], E32i[:, :, TS - 1], ["E32i"], ["Elim"])
        memset("pool", car_re[:], 0.0, ["carre%d" % q for q in range(16)]); memset("pool", car_im[:], 0.0, ["carim%d" % q for q in range(16)])
        R.barrier()
        A.release(pm2)
        alloc_stage()
        load_cast(w_glu[l], SW, SW, wglu_bf, None, "wglu")
        memset("pool", idt[:], 0.0, ["idt"])
        cp("pool", idt[:, :], tri32[:, :], ["tri32", "idt"], ["idt"])
        tt("pool", idt[:, 1:128], tri32[:, 1:128], tri32[:, 0:127], ALU.subtract, ["tri32", "idt"], ["idt"])
        for c in range(4):
            ts("dve", diagD[:, c, :], idt[:, :], dsk[:, c:c + 1], None, ALU.mult, None, ["idt", "dsk"], ["diagD"])
        R.barrier()
        A.release(pm2)
        Rb = A.alloc([128, 16, TS], F32, "Rb")
        for q in range(16):
            cp("pool", Rb[:, q, :], rdec[:, q:q + 1].to_broadcast([128, TS]), ["rdec"], ["Rb"])
        ub = [A.alloc([128, 4, TS], BF16, "ub%d" % i) for i in range(2)]
        NS = 2
        mb = [[A.alloc([128, TS], BF16, "m%d_%d" % (i, k_)) for k_ in range(4)] for i in range(NS)]
        db = [[A.alloc([128, TS], BF16, "d%d_%d" % (i, k_)) for k_ in range(4)] for i in range(NS)]
        wre = [A.alloc([128, TS], BF16, "wre%d" % i) for i in range(NS)]
        wim = [A.alloc([128, TS], BF16, "wim%d" % i) for i in range(NS)]
        z32r = [A.alloc([128, TS], F32, "z32r%d" % i) for i in range(NS)]
        z32i = [A.alloc([128, TS], F32, "z32i%d" % i) for i in range(NS)]
        zbr = [A.alloc([128, TS], BF16, "zbr%d" % i) for i in range(NS)]
        zbi = [A.alloc([128, TS], BF16, "zbi%d" % i) for i in range(NS)]
        xre = [A.alloc([128, TS], BF16, "xre%d" % i) for i in range(NS)]
        xim = [A.alloc([128, TS], BF16, "xim%d" % i) for i in range(NS)]
        cc = A.alloc([128, 8], F32, "cc")
        y32 = A.alloc([128, TS], F32, "y32"); y2 = A.alloc([128, TS], F32, "y2"); sg = A.alloc([128, TS], F32, "sg")
        g32 = A.alloc([128, 4, TS], F32, "g32"); g_bf = A.alloc([128, 4, TS], BF16, "gbf")
        s32 = A.alloc([128, 4, TS], F32, "s32"); sqs = A.alloc([128, 4, TS], BF16, "sqs")
        rstds = A.alloc([128, TS], F32, "rstds"); sn_bf = A.alloc([128, 4, TS], BF16, "snbf")
        prs = []
        for ti, (t0, n) in enumerate(tl):
            for c in range(4):
                for j in range(4):
                    prs.append((ti, t0, n, c, j))
        banks = {}
        bcnt = [0]

        def load_u(ti):
            if ti >= len(tl):
                return
            t0, n = tl[ti]; p_ = ti % 2
            R.dma("sp", ub[p_][:, :, 0:n], uT[:, t0:t0 + n].rearrange("(c p) t -> p c t", p=128), "ld_ub%d" % p_, (), ["ub%d" % p_])

        def pe_front(k):
            if k >= len(prs):
                return
            ti, t0, n, c, j = prs[k]
            p_ = ti % 2; q = 4 * c + j
            if c == 0 and j == 0:
                if ti == 0:
                    load_u(0)
                load_u(ti + 1)
            bi = (bcnt[0] % 3) * 2; bcnt[0] += 1
            banks[k] = bi
            pr, pkr = psb[bi], "ps%d" % bi
            pi, pki = psb[bi + 1], "ps%d" % (bi + 1)
            mm(pr[:, 0:n], LBre[:, q, :], ub[p_][:, c, 0:n], True, True, ["LBre", "ub%d" % p_], [pkr])
            mm(pi[:, 0:n], LBim[:, q, :], ub[p_][:, c, 0:n], True, True, ["LBim", "ub%d" % p_], [pki])

        def front_mults(k):
            ti, t0, n, c, j = prs[k]
            q = 4 * c + j; s_ = k % NS
            bi = banks[k]
            pr, pkr = psb[bi], "ps%d" % bi
            pi, pki = psb[bi + 1], "ps%d" % (bi + 1)
            m = mb[s_]; mk = ["m%d_%d" % (s_, x) for x in range(4)]
            tt("dve", m[0][:, 0:n], pr[:, 0:n], Ere[:, q, 0:n], ALU.mult, [pkr, "Ere"], [mk[0]])
            tt("dve", m[1][:, 0:n], pi[:, 0:n], Eim[:, q, 0:n], ALU.mult, [pki, "Eim"], [mk[1]])
            tt("dve", m[2][:, 0:n], pi[:, 0:n], Ere[:, q, 0:n], ALU.mult, [pki, "Ere"], [mk[2]])
            tt("dve", m[3][:, 0:n], pr[:, 0:n], Eim[:, q, 0:n], ALU.mult, [pkr, "Eim"], [mk[3]])
            tt("pool", wre[s_][:, 0:n], m[0][:, 0:n], m[1][:, 0:n], ALU.add, [mk[0], mk[1]], ["wre%d" % s_])
            tt("pool", wim[s_][:, 0:n], m[2][:, 0:n], m[3][:, 0:n], ALU.subtract, [mk[2], mk[3]], ["wim%d" % s_])

        def front_scans(k):
            ti, t0, n, c, j = prs[k]
            q = 4 * c + j; s_ = k % NS; sfx = "%d" % s_
            R.op("dve", lambda e, o=z32r[s_][:, 0:n], a=Rb[:, q, 0:n], b=wre[s_][:, 0:n], i=car_re[:, q:q + 1]:
                 e.tensor_tensor_scan(o, a, b, i, ALU.mult, ALU.add), ["Rb", "wre" + sfx, "carre%d" % q], ["z32r" + sfx])
            R.op("dve", lambda e, o=z32i[s_][:, 0:n], a=Rb[:, q, 0:n], b=wim[s_][:, 0:n], i=car_im[:, q:q + 1]:
                 e.tensor_tensor_scan(o, a, b, i, ALU.mult, ALU.add), ["Rb", "wim" + sfx, "carim%d" % q], ["z32i" + sfx])
            cp("act", zbr[s_][:, 0:n], z32r[s_][:, 0:n], ["z32r" + sfx], ["zbr" + sfx])
            cp("act", zbi[s_][:, 0:n], z32i[s_][:, 0:n], ["z32i" + sfx], ["zbi" + sfx])
            if n == TS:
                zr = z32r[s_][:, n - 1:n]; zi = z32i[s_][:, n - 1:n]
                ts("pool", cc[:, 0:1], zr, Elre[:, q:q + 1], None, ALU.mult, None, ["z32r" + sfx, "Elre"], ["cc0"])
                ts("pool", cc[:, 1:2], zi, Elim[:, q:q + 1], None, ALU.mult, None, ["z32i" + sfx, "Elim"], ["cc1"])
                ts("pool", cc[:, 2:3], zr, Elim[:, q:q + 1], None, ALU.mult, None, ["z32r" + sfx, "Elim"], ["cc2"])
                ts("pool", cc[:, 3:4], zi, Elre[:, q:q + 1], None, ALU.mult, None, ["z32i" + sfx, "Elre"], ["cc3"])
                tt("pool", car_re[:, q:q + 1], cc[:, 0:1], cc[:, 1:2], ALU.subtract, ["cc0", "cc1"], ["carre%d" % q])
                tt("pool", car_im[:, q:q + 1], cc[:, 2:3], cc[:, 3:4], ALU.add, ["cc2", "cc3"], ["carim%d" % q])

        def back(k):
            ti, t0, n, c, j = prs[k]
            p_ = ti % 2; q = 4 * c + j; s_ = k % NS; sfx = "%d" % s_
            d = db[s_]; dk = ["d%d_%d" % (s_, x) for x in range(4)]
            tt("dve", d[0][:, 0:n], zbr[s_][:, 0:n], Ere[:, q, 0:n], ALU.mult, ["zbr" + sfx, "Ere"], [dk[0]])
            tt("dve", d[1][:, 0:n], zbi[s_][:, 0:n], Eim[:, q, 0:n], ALU.mult, ["zbi" + sfx, "Eim"], [dk[1]])
            tt("dve", d[2][:, 0:n], zbr[s_][:, 0:n], Eim[:, q, 0:n], ALU.mult, ["zbr" + sfx, "Eim"], [dk[2]])
            tt("dve", d[3][:, 0:n], zbi[s_][:, 0:n], Ere[:, q, 0:n], ALU.mult, ["zbi" + sfx, "Ere"], [dk[3]])
            tt("pool", xre[s_][:, 0:n], d[0][:, 0:n], d[1][:, 0:n], ALU.subtract, [dk[0], dk[1]], ["xre" + sfx])
            tt("pool", xim[s_][:, 0:n], d[2][:, 0:n], d[3][:, 0:n], ALU.add, [dk[2], dk[3]], ["xim" + sfx])
            psy, pky = psb[6 + (c % 2)], "ps%d" % (6 + (c % 2))
            if j == 0:
                mm(psy[:, 0:n], diagD[:, c, :], ub[p_][:, c, 0:n], True, False, ["diagD", "ub%d" % p_], [pky])
            mm(psy[:, 0:n], LCre[:, q, :], xre[s_][:, 0:n], False, False, ["LCre", "xre" + sfx], [pky])
            mm(psy[:, 0:n], LCim[:, q, :], xim[s_][:, 0:n], False, j == 3, ["LCim", "xim" + sfx], [pky])
            if j == 3:
                act(y32[:, 0:n], psy[:, 0:n], AF.Identity, [pky], ["y32"])
                act(y2[:, 0:n], psy[:, 0:n], AF.Square, [pky], ["y2"])
                ts("pool", y2[:, 0:n], y2[:, 0:n], 0.044715, 1.0, ALU.mult, ALU.add, ["y2"], ["y2"])
                tt("pool", y2[:, 0:n], y2[:, 0:n], y32[:, 0:n], ALU.mult, ["y2", "y32"], ["y2"])
                act(sg[:, 0:n], y2[:, 0:n], AF.Sigmoid, ["y2"], ["sg"], scale=2.0 * math.sqrt(2.0 / math.pi))
                tt("pool", g32[:, c, 0:n], y32[:, 0:n], sg[:, 0:n], ALU.mult, ["y32", "sg"], ["g32_%d" % c])
                cp("act", g_bf[:, c, 0:n], g32[:, c, 0:n], ["g32_%d" % c], ["gbf%d" % c])
            if j == 3 and c == 3:
                for oc in range(4):
                    ps, pk = psb[6 + (oc % 2)], "ps%d" % (6 + (oc % 2))
                    for kc in range(4):
                        mm(ps[:, 0:n], wglu_bf[:, kc, 128 * oc:128 * oc + 128], g_bf[:, kc, 0:n], kc == 0, kc == 3, ["wglu", "gbf%d" % kc], [pk])
                    act(sg[:, 0:n], ps[:, 0:n], AF.Sigmoid, [pk], ["sg"])
                    tt("pool", s32[:, oc, 0:n], g32[:, oc, 0:n], sg[:, 0:n], ALU.mult, ["g32_%d" % oc, "sg"], ["s32_%d" % oc])
                    act(sqs[:, oc, 0:n], s32[:, oc, 0:n], AF.Square, ["s32_%d" % oc], ["sqs%d" % oc])
                ps, pk = psb[6], "ps6"
                for oc in range(4):
                    mm(ps[:, 0:n], ones_bf[:, :], sqs[:, oc, 0:n], oc == 0, oc == 3, ["ones", "sqs%d" % oc], [pk])
                act(rstds[:, 0:n], ps[:, 0:n], AF.Sqrt, [pk], ["rstds"], scale=1.0 / SW, bias=EPS)
                recip(rstds[:, 0:n], rstds[:, 0:n], ["rstds"], ["rstds"])
                for oc in range(4):
                    tt("pool", sn_bf[:, oc, 0:n], s32[:, oc, 0:n], rstds[:, 0:n], ALU.mult, ["s32_%d" % oc, "rstds"], ["snbf"])
                R.dma("sp", ssmT[:, t0:t0 + n].rearrange("(c p) t -> p c t", p=128), sn_bf[:, :, 0:n], "st_sn", ["snbf"], ["ssmT"])

        pe_front(0)
        for k in range(len(prs) + 1):
            pe_front(k + 1)
            if k < len(prs):
                front_mults(k)
            if k - 1 >= 0:
                back(k - 1)
            if k < len(prs):
                front_scans(k)
        R.barrier()
        if debug and debug == "B%d" % l:
            break

        A.release(base_mark)
        NKB = (L + 127) // 128
        kres = A.alloc([128, H, L], BF16, "kres"); krres = A.alloc([64, L], BF16, "krres")
        vres = A.alloc([128, NKB, 512], BF16, "vres")
        for h in range(H):
            R.dma("sp", kres[:, h, :], kT[h], "ld_kres", (), ["kres"])
        R.dma("sp", krres[:, :], krT[:, :], "ld_kres", (), ["kres"])
        nfull = L // 128
        R.dma("sp", vres[:, 0:nfull, :], vv[0:nfull * 128, :].rearrange("(b p) d -> p b d", p=128), "ld_kres", (), ["kres"])
        if L % 128:
            R.dma("sp", vres[0:L % 128, nfull, :], vv[nfull * 128:L, :], "ld_kres", (), ["kres"])
        qn = [A.alloc([128, H, TS], BF16, "qn%d" % i) for i in range(2)]
        qr = [A.alloc([64, H, TS], BF16, "qr%d" % i) for i in range(2)]
        NPT = 4; LA = 2
        pT = [A.alloc([128, TS], BF16, "pT%d" % i) for i in range(NPT)]
        rinv = [A.alloc([128, TS], F32, "rinv%d" % i) for i in range(2)]
        at32 = [A.alloc([128, H, TS], F32, "at32_%d" % i) for i in range(2)]
        sqa = [A.alloc([128, H, TS], BF16, "sqa%d" % i) for i in range(2)]
        rstda = A.alloc([128, TS], F32, "rstda"); an_bf = A.alloc([128, H, TS], BF16, "anbf")
        units = []
        for ti, (t0, n) in enumerate(tl):
            kbmax = (t0 + n - 1) // 128
            for h in range(H):
                for kb in range(kbmax + 1):
                    units.append((ti, t0, n, h, kb, kbmax))
        scnt = [0]
        slot_of = {}

        def load_q(ti):
            if ti >= len(tl):
                return
            t0, n = tl[ti]; p_ = ti % 2
            R.dma("sp", qn[p_][:, :, 0:n], qnT[:, :, t0:t0 + n].rearrange("h p t -> p h t"), "ld_qn%d" % p_, (), ["qn%d" % p_])
            R.dma("sp", qr[p_][:, :, 0:n], qrT[:, :, t0:t0 + n].rearrange("h p t -> p h t"), "ld_qr%d" % p_, (), ["qr%d" % p_])

        def emit_S(ui):
            ti, t0, n, h, kb, kbmax = units[ui]
            p_ = ti % 2
            if h == 0 and kb == 0:
                if ti == 0:
                    load_q(0)
                load_q(ti + 1)
            ks = min(128, L - 128 * kb); c0 = max(0, 128 * kb - t0)
            sb = scnt[0] % NPT; scnt[0] += 1
            slot_of[ui] = sb
            pss, pks = psb[sb], "ps%d" % sb
            mm(pss[0:ks, c0:n], kres[:, h, 128 * kb:128 * kb + ks], qn[p_][:, h, c0:n], True, False, ["kres", "qn%d" % p_], [pks])
            mm(pss[0:ks, c0:n], krres[:, 128 * kb:128 * kb + ks], qr[p_][:, h, c0:n], False, True, ["kres", "qr%d" % p_], [pks])
            pk_ = "pT%d" % sb
            act(pT[sb][0:ks, c0:n], pss[0:ks, c0:n], AF.Exp, [pks], [pk_], scale=SCALE)
            if 128 * kb >= t0:
                w = min(128, n - c0)
                tt("pool", pT[sb][0:ks, c0:c0 + w], pT[sb][0:ks, c0:c0 + w], tri_bf[0:ks, 0:w], ALU.mult, [pk_, "tri"], [pk_])

        def emit_PV(ui):
            ti, t0, n, h, kb, kbmax = units[ui]
            p_ = ti % 2
            ks = min(128, L - 128 * kb); c0 = max(0, 128 * kb - t0)
            sb = slot_of[ui]; pk_ = "pT%d" % sb
            po, pko = psb[4 + (h % 2)], "ps%d" % (4 + (h % 2))
            prr, pkrr = psb[6 + (h % 2)], "ps%d" % (6 + (h % 2))
            last = kb == kbmax
            mm(po[:, c0:n], vres[0:ks, kb, 128 * h:128 * h + 128], pT[sb][0:ks, c0:n], kb == 0, last, ["kres", pk_], [pko])
            mm(prr[:, c0:n], ones_bf[0:ks, :], pT[sb][0:ks, c0:n], kb == 0, last, ["ones", pk_], [pkrr])
            if last:
                hp = h % 2
                recip(rinv[hp][:, 0:n], prr[:, 0:n], [pkrr], ["rinv%d" % hp])
                tt("dve", at32[p_][:, h, 0:n], po[:, 0:n], rinv[hp][:, 0:n], ALU.mult, [pko, "rinv%d" % hp], ["at32_%d_%d" % (p_, h)])
                act(sqa[p_][:, h, 0:n], at32[p_][:, h, 0:n], AF.Square, ["at32_%d_%d" % (p_, h)], ["sqa%d_%d" % (p_, h)])
                if h == H - 1:
                    sb2 = scnt[0] % NPT; scnt[0] += 1
                    ps, pk = psb[sb2], "ps%d" % sb2
                    for hh_ in range(H):
                        mm(ps[:, 0:n], ones_bf[:, :], sqa[p_][:, hh_, 0:n], hh_ == 0, hh_ == H - 1, ["ones", "sqa%d_%d" % (p_, hh_)], [pk])
                    act(rstda[:, 0:n], ps[:, 0:n], AF.Sqrt, [pk], ["rstda"], scale=1.0 / SW, bias=EPS)
                    recip(rstda[:, 0:n], rstda[:, 0:n], ["rstda"], ["rstda"])
                    for hh_ in range(H):
                        tt("pool", an_bf[:, hh_, 0:n], at32[p_][:, hh_, 0:n], rstda[:, 0:n], ALU.mult, ["at32_%d_%d" % (p_, hh_), "rstda"], ["anbf"])
                    R.dma("sp", attT[:, t0:t0 + n].rearrange("(c p) t -> p c t", p=128), an_bf[:, :, 0:n], "st_an", ["anbf"], ["attT"])

        for idx in range(len(units) + LA):
            if idx < len(units):
                emit_S(idx)
            if idx - LA >= 0:
                emit_PV(idx - LA)
        R.barrier()
        if debug and debug == "C%d" % l:
            break

        A.release(base_mark)
        T2 = 256
        gao = A.alloc([128, 8], F32, "gao")
        R.dma("sp", gao[:, 0:4], g_ao[l], "c_gao", (), ["wo_g"])
        R.dma("sp", gao[:, 4:8], g_so[l], "c_gao2", (), ["wo_g"])
        gff = A.alloc([128, DC], F32, "gff"); R.dma("sp", gff[:], g_ffn[l], "c_gff", (), ["wg_g", "wu_g"])
        wo_bf = A.alloc([128, 8, D], BF16, "wo"); wg_bf = A.alloc([128, DC, FF], BF16, "wg")
        wu_bf = A.alloc([128, DC, FF], BF16, "wu"); wd_bf = A.alloc([128, FC, D], BF16, "wd")
        pm3 = A.mark()
        alloc_stage()
        load_cast(w_o[l], D, D, wo_bf, gao, "wo")
        load_cast(w_gate[l], D, FF, wg_bf, gff, "wg")
        load_cast(w_up[l], D, FF, wu_bf, gff, "wu")
        load_cast(w_down[l], FF, D, wd_bf, None, "wd")
        R.barrier()
        A.release(pm3)
        hhb = [A.alloc([128, DC, T2], F32, "hh%d" % i) for i in range(2)]
        mixb = [A.alloc([128, 8, T2], BF16, "mix%d" % i) for i in range(2)]
        xb2 = A.alloc([128, DC, T2], BF16, "xb2"); sq2 = A.alloc([128, DC, T2], BF16, "sq2")
        rstdf = A.alloc([128, T2], F32, "rstdf")
        gs = A.alloc([128, T2], F32, "gs"); us = A.alloc([128, T2], F32, "us"); sl_ = A.alloc([128, T2], F32, "sl")
        a_bf = A.alloc([128, FC, T2], BF16, "abf")
        tl2 = tiles_of(L, T2)

        def load_c2(ti):
            if ti >= len(tl2):
                return
            t0, n = tl2[ti]; p_ = ti % 2
            R.dma("sp", hhb[p_][:, :, 0:n], hT[:, t0:t0 + n].rearrange("(c p) t -> p c t", p=128), "ld_hh%d" % p_, (),
                  ["hh_%d" % p_] + ["hh_%d_%d" % (p_, oc) for oc in range(DC)])
            R.dma("sp", mixb[p_][:, 0:4, 0:n], attT[:, t0:t0 + n].rearrange("(c p) t -> p c t", p=128), "ld_mix%d" % p_, (), ["mixa%d" % p_])
            R.dma("sp", mixb[p_][:, 4:8, 0:n], ssmT[:, t0:t0 + n].rearrange("(c p) t -> p c t", p=128), "ld_mixs%d" % p_, (), ["mixs%d" % p_])

        load_c2(0)
        for ti, (t0, n) in enumerate(tl2):
            p_ = ti % 2
            hh = hhb[p_]; mix = mixb[p_]
            hk = lambda oc: "hh_%d_%d" % (p_, oc)
            load_c2(ti + 1)
            for oc in range(DC):
                ps, pk = nextps()
                for kc in range(8):
                    mm(ps[:, 0:n], wo_bf[:, kc, 128 * oc:128 * oc + 128], mix[:, kc, 0:n], kc == 0, kc == 7, ["wo", "mixa%d" % p_, "mixs%d" % p_], [pk])
                tt("dve", hh[:, oc, 0:n], ps[:, 0:n], hh[:, oc, 0:n], ALU.add, [pk, hk(oc)], [hk(oc)])
                cp("pool", xb2[:, oc, 0:n], hh[:, oc, 0:n], [hk(oc)], ["xb2_%d" % oc])
                act(sq2[:, oc, 0:n], hh[:, oc, 0:n], AF.Square, [hk(oc)], ["sq2_%d" % oc])
            rms_rstd([(sq2[:, c, 0:n], "sq2_%d" % c) for c in range(DC)], D, n, "F", rstdf[:, 0:n], "rstdf")
            for fc in range(FC):
                pg, pkg = nextps(); pu, pku = nextps()
                for c in range(DC):
                    mm(pg[:, 0:n], wg_bf[:, c, 128 * fc:128 * fc + 128], xb2[:, c, 0:n], c == 0, c == DC - 1, ["wg", "xb2_%d" % c], [pkg])
                for c in range(DC):
                    mm(pu[:, 0:n], wu_bf[:, c, 128 * fc:128 * fc + 128], xb2[:, c, 0:n], c == 0, c == DC - 1, ["wu", "xb2_%d" % c], [pku])
                tt("dve", gs[:, 0:n], pg[:, 0:n], rstdf[:, 0:n], ALU.mult, [pkg, "rstdf"], ["gs"])
                tt("dve", us[:, 0:n], pu[:, 0:n], rstdf[:, 0:n], ALU.mult, [pku, "rstdf"], ["us"])
                act(sl_[:, 0:n], gs[:, 0:n], AF.Silu, ["gs"], ["sl"])
                tt("pool", a_bf[:, fc, 0:n], sl_[:, 0:n], us[:, 0:n], ALU.mult, ["sl", "us"], ["abf%d" % fc])
            last_layer = (l == DEPTH - 1)
            for oc in range(DC):
                ps, pk = nextps()
                for fc in range(FC):
                    mm(ps[:, 0:n], wd_bf[:, fc, 128 * oc:128 * oc + 128], a_bf[:, fc, 0:n], fc == 0, fc == FC - 1, ["wd", "abf%d" % fc], [pk])
                tt("dve", hh[:, oc, 0:n], ps[:, 0:n], hh[:, oc, 0:n], ALU.add, [pk, hk(oc)], [hk(oc)])
                if last_layer:
                    act(sq2[:, oc, 0:n], hh[:, oc, 0:n], AF.Square, [hk(oc)], ["sq2_%d" % oc])
            allh = ["hh_%d" % p_] + [hk(oc) for oc in range(DC)]
            if not last_layer:
                R.dma("sp", hT[:, t0:t0 + n].rearrange("(c p) t -> p c t", p=128), hh[:, :, 0:n], "st_hh%d" % p_, allh, ["hT"])
            else:
                rms_rstd([(sq2[:, c, 0:n], "sq2_%d" % c) for c in range(DC)], D, n, "G", rstdf[:, 0:n], "rstdf")
                for oc in range(DC):
                    R.op("dve", lambda e, o=hh[:, oc, 0:n], a=hh[:, oc, 0:n], s=gfin[:, oc:oc + 1], b=rstdf[:, 0:n]:
                         e.scalar_tensor_tensor(o, a, s, b, ALU.mult, ALU.mult), [hk(oc), "rstdf", "gfin"], [hk(oc)])
                a0 = max(t0, NM)
                if t0 + n > a0:
                    R.dma("sp", outT[:, a0 - NM:t0 + n - NM].rearrange("(c p) t -> p c t", p=128), hh[:, :, a0 - t0:n], "st_o%d" % p_,
                          allh, ["outT"])
        R.barrier()

    R.emit(nc, stack)
    stack.close()
    return nc


def _lay_vec(v, C):
    return np.ascontiguousarray(v.reshape(C, 128).T)


def host_layout(inp, LREAL):
    L = LREAL + NM
    f = np.float32
    out = {}
    out["metaT"] = np.ascontiguousarray(inp["meta_tokens"].T.astype(f))
    pos = np.arange(L, dtype=f)
    inv = (1.0 / (np.float32(10000.0) ** (np.arange(0, DR, 2, dtype=f) / np.float32(DR)))).astype(f)
    ang = (pos[:, None] * inv[None, :]).astype(f)
    c = np.cos(ang).astype(f).T; s = np.sin(ang).astype(f).T
    out["cosT"] = np.ascontiguousarray(np.concatenate([c, c], 0)); out["sinT"] = np.ascontiguousarray(np.concatenate([s, s], 0))
    k = np.arange(128)
    out["trim"] = (k[None, :] >= k[:, None]).astype(f)
    out["g_mix"] = np.stack([_lay_vec(inp["norm_mix_g"][l], DC) for l in range(DEPTH)])
    out["g_ffn"] = np.stack([_lay_vec(inp["norm_ffn_g"][l], DC) for l in range(DEPTH)])
    out["g_fin"] = _lay_vec(inp["final_norm_g"], DC)
    out["g_q"] = np.stack([_lay_vec(inp["q_norm_g"][l], 3) for l in range(DEPTH)])
    out["g_kv"] = np.stack([_lay_vec(inp["kv_norm_g"][l], 2) for l in range(DEPTH)])
    out["g_ao"] = np.stack([_lay_vec(inp["attn_out_g"][l], 4) for l in range(DEPTH)])
    out["g_so"] = np.stack([_lay_vec(inp["ssm_out_g"][l], 4) for l in range(DEPTH)])
    out["d_skip"] = np.stack([_lay_vec(inp["ssm_d"][l], 4) for l in range(DEPTH)])
    for nm in ["w_in", "w_uq", "w_ukv", "w_glu", "w_o", "w_gate", "w_up", "w_down"]:
        out[nm] = np.ascontiguousarray(inp[nm].astype(f))
    aR_re = np.zeros((DEPTH, 128, 4, 128), f); aR_im = np.zeros_like(aR_re); ldtR = np.zeros_like(aR_re)
    bR_re = np.zeros((DEPTH, 128, 16, 128), f); bR_im = np.zeros_like(bR_re)
    aS_re = np.zeros((DEPTH, 128, 16), f); aS_im = np.zeros_like(aS_re); ldtS = np.zeros_like(aS_re)
    cS_re = np.zeros((DEPTH, 128, 16, 128), f); cS_im = np.zeros_like(cS_re)
    for l in range(DEPTH):
        are, aim, ldt = inp["ssm_a_re"][l], inp["ssm_a_im"][l], inp["ssm_log_dt"][l]
        bre, bim, cre, cim = inp["ssm_b_re"][l], inp["ssm_b_im"][l], inp["ssm_c_re"][l], inp["ssm_c_im"][l]
        for q in range(16):
            for gl in range(2):
                g = 2 * q + gl
                aS_re[l, 64 * gl:64 * gl + 64, q] = are[g]; aS_im[l, 64 * gl:64 * gl + 64, q] = aim[g]
                ldtS[l, 64 * gl:64 * gl + 64, q] = ldt[g]
                j = q % 4
                cS_re[l, 64 * gl:64 * gl + 64, q, 32 * j + 16 * gl:32 * j + 16 * gl + 16] = cre[g].T
                cS_im[l, 64 * gl:64 * gl + 64, q, 32 * j + 16 * gl:32 * j + 16 * gl + 16] = cim[g].T
                bR_re[l, 32 * j + 16 * gl:32 * j + 16 * gl + 16, q, 64 * gl:64 * gl + 64] = bre[g].T
                bR_im[l, 32 * j + 16 * gl:32 * j + 16 * gl + 16, q, 64 * gl:64 * gl + 64] = bim[g].T
        for c in range(4):
            for j in range(4):
                for gl in range(2):
                    g = 8 * c + 2 * j + gl
                    aR_re[l, 32 * j:32 * j + 32, c, 64 * gl:64 * gl + 64] = are[g][None, :]
                    aR_im[l, 32 * j:32 * j + 32, c, 64 * gl:64 * gl + 64] = aim[g][None, :]
                    ldtR[l, 32 * j:32 * j + 32, c, 64 * gl:64 * gl + 64] = ldt[g]
    out.update(aR_re=aR_re, aR_im=aR_im, ldtR=ldtR, bR_re=bR_re, bR_im=bR_im, aS_re=aS_re, aS_im=aS_im, ldtS=ldtS,
               cS_re=cS_re, cS_im=cS_im)
    return out


_CACHE = {}


def run(inputs, debug=False, trace=False):
    x = np.asarray(inputs["x"], dtype=np.float32)
    B, LREAL, _ = x.shape
    key = (LREAL, debug)
    if key not in _CACHE:
        _CACHE[key] = build_program(LREAL, debug)
    nc = _CACHE[key]
    inp = {k: np.asarray(v) for k, v in inputs.items()}
    common = host_layout(inp, LREAL)
    in_maps = []
    for b in range(B):
        m = dict(common)
        m["xT"] = np.ascontiguousarray(x[b].T)
        in_maps.append(m)
    res = run_bass_kernel_spmd(nc, in_maps, core_ids=list(range(B)), trace=trace)
    return res


def kernel(**inputs):
    res = run(inputs)
    out = np.stack([np.ascontiguousarray(r["outT"].T) for r in res.results], 0)
    return out.astype(np.float32)
```
